# Optimizing a Trainium2 kernel written in Bass

```python
import math
import jax, jax.numpy as jnp
from jax import lax
import numpy as np

D_MODEL = 1024
BATCH = 8
SEQ = 2048
DEPTH = 4

GRID_W = 64
CTX_LEN = 256
HEAD_DIM = 64
Q_BLOCK = 128
ROPE_THETA = 10000.0
RMS_EPS = 1e-6
N_MOD = 9
N_BRANCH = 4
BRANCH_W = D_MODEL // 2
FFN_DIM = 256 * ((8 * D_MODEL // 3 + 255) // 256)
GQA_Q_HEADS = BRANCH_W // HEAD_DIM
GQA_KV_HEADS = 2
GQA_GROUP = GQA_Q_HEADS // GQA_KV_HEADS
GQA_SCALE = HEAD_DIM ** -0.5
HY_W = BRANCH_W
HY_ORDER = 2
HY_IN = (HY_ORDER + 1) * HY_W
HY_CONV = 3
HY_BANDS = 16
HY_EMB = 1 + 2 * HY_BANDS
HY_HIDDEN = 64
HY_DECAY_TARGET = 1e-2
HY_FAST_DECAY = 0.3
HY_SLOW_DECAY = 1.5
HY_MIN_DECAY = math.log(HY_DECAY_TARGET) / HY_SLOW_DECAY
HY_MAX_DECAY = math.log(HY_DECAY_TARGET) / HY_FAST_DECAY
NA_HEADS = BRANCH_W // HEAD_DIM
NA_ROWS = 8
NA_COLS = 16
NA_SCALE = HEAD_DIM ** -0.5
MLA_HEADS = 8
MLA_NOPE = 64
MLA_ROPE = 32
MLA_V = BRANCH_W // MLA_HEADS
MLA_Q_RANK = 3 * D_MODEL // 8
MLA_KV_RANK = D_MODEL // 4
MLA_SCALE = (MLA_NOPE + MLA_ROPE) ** -0.5
SPLIT_SIZES = (GQA_Q_HEADS * HEAD_DIM, GQA_KV_HEADS * HEAD_DIM, GQA_KV_HEADS * HEAD_DIM,
               HY_IN,
               NA_HEADS * HEAD_DIM, NA_HEADS * HEAD_DIM, NA_HEADS * HEAD_DIM,
               MLA_Q_RANK, MLA_KV_RANK, MLA_ROPE,
               N_BRANCH * D_MODEL)
SPLIT_OFFSETS = [int(v) for v in np.cumsum(SPLIT_SIZES)[:-1]]
IN_COLS = int(sum(SPLIT_SIZES))

kernel_name = 'hybrid_gated_multimixer_dit'


def rms_norm(x, gain=None):
    xf = x.astype(jnp.float32)
    y = (xf * lax.rsqrt(jnp.mean(xf * xf, axis=-1, keepdims=True) + RMS_EPS)).astype(x.dtype)
    return y if gain is None else y * gain


def modulate(x, shift, scale):
    return x * (1 + scale) + shift


def adaln(cond, w, b):
    mod = cond @ w + b
    return jnp.split(mod[:, None, :], N_MOD, axis=-1)


def swiglu(x, w_up, w_down):
    a, g = jnp.split(x @ w_up, 2, axis=-1)
    return (jax.nn.silu(a) * g) @ w_down


def split_heads(x, n_heads):
    return x.reshape(*x.shape[:-1], n_heads, x.shape[-1] // n_heads)


def rope_1d(x, pos):
    half = x.shape[-1] // 2
    freqs = ROPE_THETA ** (-jnp.arange(half, dtype=jnp.float32) / half)
    ang = pos.astype(jnp.float32)[:, None] * freqs
    cos = jnp.cos(ang)[:, None, :].astype(x.dtype)
    sin = jnp.sin(ang)[:, None, :].astype(x.dtype)
    x1, x2 = x[..., :half], x[..., half:]
    return jnp.concatenate([x1 * cos - x2 * sin, x2 * cos + x1 * sin], axis=-1)


def rope_2d(x, rows, cols):
    half = x.shape[-1] // 2
    return jnp.concatenate([rope_1d(x[..., :half], rows), rope_1d(x[..., half:], cols)], axis=-1)


def sweep_query_blocks(fn, qs):
    b, s = qs[0].shape[:2]
    blk = min(Q_BLOCK, s)
    nb = s // blk
    xs = tuple(a.reshape(b, nb, blk, *a.shape[2:]).swapaxes(0, 1) for a in qs)
    out = lax.map(lambda blocks: fn(*blocks), xs)
    return out.swapaxes(0, 1).reshape(b, s, *out.shape[3:])


def attend_gqa(q, k, v, scale):
    s = jnp.einsum('bqhgd,bkhd->bhgqk', q, k).astype(jnp.float32) * scale
    p = jax.nn.softmax(s, axis=-1).astype(v.dtype)
    return jnp.einsum('bhgqk,bkhd->bqhgd', p, v)


def gqa_branch(q_l, k_l, v_l, q_c, k_c, v_c, q_gain, k_gain, rows, cols, with_ctx):
    b, s = q_l.shape[:2]
    q = rope_2d(rms_norm(split_heads(q_l, GQA_Q_HEADS), q_gain), rows, cols)
    q = q.reshape(b, s, GQA_KV_HEADS, GQA_GROUP, HEAD_DIM)
    k = rope_2d(rms_norm(split_heads(k_l, GQA_KV_HEADS), k_gain), rows, cols)
    kc = rms_norm(split_heads(k_c, GQA_KV_HEADS), k_gain)
    vc = split_heads(v_c, GQA_KV_HEADS)
    k_all = jnp.concatenate([k, kc], axis=1)
    v_all = jnp.concatenate([split_heads(v_l, GQA_KV_HEADS), vc], axis=1)
    o = sweep_query_blocks(lambda qb: attend_gqa(qb, k_all, v_all, GQA_SCALE), (q,)).reshape(b, s, -1)
    if not with_ctx:
        return o, None
    n_c = q_c.shape[1]
    qc = rms_norm(split_heads(q_c, GQA_Q_HEADS), q_gain).reshape(b, n_c, GQA_KV_HEADS, GQA_GROUP, HEAD_DIM)
    return o, attend_gqa(qc, kc, vc, GQA_SCALE).reshape(b, n_c, -1)


def short_conv(u, w, bias):
    ch = u.shape[-1]
    y = lax.conv_general_dilated(u, w[:, None, :], window_strides=(1,),
                                 padding=[(HY_CONV // 2, HY_CONV // 2)],
                                 dimension_numbers=('NWC', 'WIO', 'NWC'),
                                 feature_group_count=ch)
    return y + bias


def hyena_filter(length, f_w1, f_b, f_freq, f_w_mid, f_w_out):
    t = jnp.linspace(0.0, 1.0, length, dtype=jnp.float32)[:, None]
    w = (2.0 * math.pi / length) * jnp.arange(length, dtype=jnp.float32)[:, None]
    f = jnp.linspace(1e-4, HY_BANDS - 1, HY_BANDS, dtype=jnp.float32)[None, :]
    z = jnp.concatenate([t, jnp.cos(f * w), -jnp.sin(f * w)], axis=-1).astype(f_w1.dtype)
    hdn = jnp.sin(f_freq[0] * (z @ f_w1 + f_b[0]))
    hdn = jnp.sin(f_freq[1] * (hdn @ f_w_mid[0] + f_b[1]))
    hdn = jnp.sin(f_freq[2] * (hdn @ f_w_mid[1] + f_b[2]))
    filt = (hdn @ f_w_out).astype(jnp.float32).reshape(length, 2, HY_W)
    deltas = jnp.abs(jnp.linspace(HY_MIN_DECAY, HY_MAX_DECAY, HY_W, dtype=jnp.float32))
    filt = filt * jnp.exp(-t * deltas)[:, None, :]
    k_full = jnp.concatenate([filt[:, 0], jnp.zeros((1, HY_W), jnp.float32), filt[:0:-1, 1]], axis=0)
    return k_full / jnp.sum(jnp.abs(k_full), axis=0, keepdims=True)


def fft_conv(v, k_full):
    length = v.shape[1]
    n = 2 * length
    vf = jnp.fft.rfft(v.astype(jnp.float32), n=n, axis=1)
    kf = jnp.fft.rfft(k_full, n=n, axis=0)
    y = jnp.fft.irfft(vf * kf[None], n=n, axis=1)[:, :length]
    return y.astype(v.dtype)


def hyena_mix(u, conv_w, conv_b, f_w1, f_b, f_freq, f_w_mid, f_w_out, skip):
    u = short_conv(u, conv_w, conv_b)
    x0, x1, v = jnp.split(u, HY_ORDER + 1, axis=-1)
    k_full = hyena_filter(u.shape[1], f_w1, f_b, f_freq, f_w_mid, f_w_out)
    v = v * x1
    y = fft_conv(v, k_full) + v * skip
    return y * x0


def na_latent(q, k, v, k_ctx, v_ctx, rpb):
    b, s, nh, hd = q.shape
    n_rows = s // GRID_W
    kr = min(NA_ROWS, n_rows)
    n_loc = kr * GRID_W
    qc = jnp.arange(GRID_W)[:, None]
    kc = jnp.arange(GRID_W)[None, :]
    col_start = jnp.clip(qc - NA_COLS // 2, 0, GRID_W - NA_COLS)
    col_ok = (kc >= col_start) & (kc < col_start + NA_COLS)
    col_idx = jnp.clip(kc - qc + NA_COLS - 1, 0, 2 * NA_COLS - 2)
    q_rows = q.reshape(b, n_rows, GRID_W, nh, hd).swapaxes(0, 1)

    def one_row(args):
        q_r, r = args
        start = jnp.clip(r - kr // 2, 0, n_rows - kr)
        k_loc = lax.dynamic_slice_in_dim(k, start * GRID_W, n_loc, axis=1)
        v_loc = lax.dynamic_slice_in_dim(v, start * GRID_W, n_loc, axis=1)
        row_idx = start + jnp.arange(kr) - r + NA_ROWS - 1
        bias = rpb[:, row_idx][:, :, col_idx].astype(jnp.float32)
        bias = jnp.where(col_ok, bias, -jnp.inf).transpose(0, 2, 1, 3).reshape(nh, GRID_W, n_loc)
        s_loc = jnp.einsum('bqhd,bkhd->bhqk', q_r, k_loc).astype(jnp.float32) * NA_SCALE + bias
        s_ctx = jnp.einsum('bqhd,bkhd->bhqk', q_r, k_ctx).astype(jnp.float32) * NA_SCALE
        p = jax.nn.softmax(jnp.concatenate([s_loc, s_ctx], axis=-1), axis=-1).astype(v.dtype)
        return (jnp.einsum('bhqk,bkhd->bqhd', p[..., :n_loc], v_loc)
                + jnp.einsum('bhqk,bkhd->bqhd', p[..., n_loc:], v_ctx))

    out = lax.map(one_row, (q_rows, jnp.arange(n_rows)))
    return out.swapaxes(0, 1).reshape(b, s, nh * hd)


def na_branch(q_l, k_l, v_l, q_c, k_c, v_c, rpb, with_ctx):
    kc = split_heads(k_c, NA_HEADS)
    vc = split_heads(v_c, NA_HEADS)
    o = na_latent(split_heads(q_l, NA_HEADS), split_heads(k_l, NA_HEADS), split_heads(v_l, NA_HEADS), kc, vc, rpb)
    if not with_ctx:
        return o, None
    b, n_c = q_c.shape[:2]
    qc = split_heads(q_c, NA_HEADS)[:, :, :, None, :]
    return o, attend_gqa(qc, kc, vc, NA_SCALE).reshape(b, n_c, -1)


def attend_mla(q_nope, q_rope, k_nope, k_rope, v):
    s = (jnp.einsum('bqhd,bkhd->bhqk', q_nope, k_nope)
         + jnp.einsum('bqhr,bkr->bhqk', q_rope, k_rope)).astype(jnp.float32) * MLA_SCALE
    p = jax.nn.softmax(s, axis=-1).astype(v.dtype)
    return jnp.einsum('bhqk,bkhd->bqhd', p, v)


def mla_branch(qa, kva, kr, qa_c, kva_c, kr_c, q_gain, kv_gain, wq_b, wkv_b, rows, cols, with_ctx):
    def queries(a):
        qh = split_heads(rms_norm(a, q_gain) @ wq_b, MLA_HEADS)
        return qh[..., :MLA_NOPE], qh[..., MLA_NOPE:]

    def keys_values(a):
        kvh = split_heads(rms_norm(a, kv_gain) @ wkv_b, MLA_HEADS)
        return kvh[..., :MLA_NOPE], kvh[..., MLA_NOPE:]

    b, s = qa.shape[:2]
    q_nope, q_rope = queries(qa)
    q_rope = rope_2d(q_rope, rows, cols)
    k_nope, v = keys_values(kva)
    k_rope = rope_2d(kr[:, :, None, :], rows, cols)[:, :, 0]
    kn_c, v_c = keys_values(kva_c)
    kn_all = jnp.concatenate([k_nope, kn_c], axis=1)
    kr_all = jnp.concatenate([k_rope, kr_c], axis=1)
    v_all = jnp.concatenate([v, v_c], axis=1)
    o = sweep_query_blocks(lambda qn, qr: attend_mla(qn, qr, kn_all, kr_all, v_all), (q_nope, q_rope)).reshape(b, s, -1)
    if not with_ctx:
        return o, None
    qn_c, qr_c = queries(qa_c)
    return o, attend_mla(qn_c, qr_c, kn_c, kr_c, v_c).reshape(b, qa_c.shape[1], -1)


def merge_branches(outs, gate_logits, w_branch, w_out):
    b, t = gate_logits.shape[:2]
    gates = jax.nn.sigmoid(gate_logits.reshape(b, t, N_BRANCH, D_MODEL))
    merged = gates[:, :, 0] * (outs[0] @ w_branch[0])
    for i in range(1, N_BRANCH):
        merged = merged + gates[:, :, i] * (outs[i] @ w_branch[i])
    return merged @ w_out


def token_mix(n, nc, rows, cols, w_in, gqa_p, hy_p, na_rpb, mla_p, w_branch, w_out, with_ctx):
    (gq, gk, gv, hy_u, na_q, na_k, na_v, m_qa, m_kva, m_kr, gate) = jnp.split(n @ w_in, SPLIT_OFFSETS, axis=-1)
    (gq_c, gk_c, gv_c, hy_u_c, na_q_c, na_k_c, na_v_c, m_qa_c, m_kva_c, m_kr_c, gate_c) = jnp.split(nc @ w_in, SPLIT_OFFSETS, axis=-1)
    o_gqa, c_gqa = gqa_branch(gq, gk, gv, gq_c, gk_c, gv_c, gqa_p[0], gqa_p[1], rows, cols, with_ctx)
    o_hy = hyena_mix(hy_u, *hy_p)
    o_na, c_na = na_branch(na_q, na_k, na_v, na_q_c, na_k_c, na_v_c, na_rpb, with_ctx)
    o_mla, c_mla = mla_branch(m_qa, m_kva, m_kr, m_qa_c, m_kva_c, m_kr_c, *mla_p, rows, cols, with_ctx)
    out = merge_branches((o_gqa, o_hy, o_na, o_mla), gate, w_branch, w_out)
    if not with_ctx:
        return out, None
    c_hy = hyena_mix(hy_u_c, *hy_p)
    out_c = merge_branches((c_gqa, c_hy, c_na, c_mla), gate_c, w_branch, w_out)
    return out, out_c


def setup_inputs(seed: int = 0) -> dict:
    key = jax.random.key(seed)
    ks = jax.random.split(key, 32)
    f32 = jnp.float32

    def nrm(k, shape, scale):
        return jax.random.normal(k, shape, f32) * scale

    D = D_MODEL
    return {
        'x': nrm(ks[0], (BATCH, SEQ, D), 1.0),
        'c': nrm(ks[1], (BATCH, D), 1.0),
        'ctx': nrm(ks[2], (BATCH, CTX_LEN, D), 1.0),
        'c_ctx': nrm(ks[3], (D,), 1.0),
        'ada_w': nrm(ks[4], (DEPTH, D, N_MOD * D), D ** -0.5),
        'ada_b': nrm(ks[5], (DEPTH, N_MOD * D), 0.02),
        'ffn1_up': nrm(ks[6], (DEPTH, D, 2 * FFN_DIM), D ** -0.5),
        'ffn1_down': nrm(ks[7], (DEPTH, FFN_DIM, D), FFN_DIM ** -0.5),
        'w_in': nrm(ks[8], (DEPTH, D, IN_COLS), D ** -0.5),
        'gqa_q_norm': 1.0 + nrm(ks[9], (DEPTH, HEAD_DIM), 0.05),
        'gqa_k_norm': 1.0 + nrm(ks[10], (DEPTH, HEAD_DIM), 0.05),
        'hy_conv_w': nrm(ks[11], (DEPTH, HY_CONV, HY_IN), HY_CONV ** -0.5),
        'hy_conv_b': nrm(ks[12], (DEPTH, HY_IN), 0.02),
        'hy_f_w1': nrm(ks[13], (DEPTH, HY_EMB, HY_HIDDEN), HY_EMB ** -0.5),
        'hy_f_b': nrm(ks[14], (DEPTH, 3, HY_HIDDEN), 0.1),
        'hy_f_freq': 1.0 + nrm(ks[15], (DEPTH, 3, HY_HIDDEN), 0.05),
        'hy_f_w_mid': nrm(ks[16], (DEPTH, 2, HY_HIDDEN, HY_HIDDEN), HY_HIDDEN ** -0.5),
        'hy_f_w_out': nrm(ks[17], (DEPTH, HY_HIDDEN, 2 * HY_W), HY_HIDDEN ** -0.5),
        'hy_skip': nrm(ks[18], (DEPTH, HY_W), 0.5),
        'na_rpb': nrm(ks[19], (DEPTH, NA_HEADS, 2 * NA_ROWS - 1, 2 * NA_COLS - 1), 0.1),
        'mla_q_norm': 1.0 + nrm(ks[20], (DEPTH, MLA_Q_RANK), 0.05),
        'mla_kv_norm': 1.0 + nrm(ks[21], (DEPTH, MLA_KV_RANK), 0.05),
        'mla_wq_b': nrm(ks[22], (DEPTH, MLA_Q_RANK, MLA_HEADS * (MLA_NOPE + MLA_ROPE)), MLA_Q_RANK ** -0.5),
        'mla_wkv_b': nrm(ks[23], (DEPTH, MLA_KV_RANK, MLA_HEADS * (MLA_NOPE + MLA_V)), MLA_KV_RANK ** -0.5),
        'w_branch': nrm(ks[24], (DEPTH, N_BRANCH, BRANCH_W, D), BRANCH_W ** -0.5),
        'w_out': nrm(ks[25], (DEPTH, D, D), D ** -0.5),
        'ffn2_up': nrm(ks[26], (DEPTH, D, 2 * FFN_DIM), D ** -0.5),
        'ffn2_down': nrm(ks[27], (DEPTH, FFN_DIM, D), FFN_DIM ** -0.5),
        'final_norm': 1.0 + nrm(ks[28], (D,), 0.05),
    }


def reference(x, c, ctx, c_ctx, ada_w, ada_b, ffn1_up, ffn1_down, w_in, gqa_q_norm, gqa_k_norm,
              hy_conv_w, hy_conv_b, hy_f_w1, hy_f_b, hy_f_freq, hy_f_w_mid, hy_f_w_out, hy_skip,
              na_rpb, mla_q_norm, mla_kv_norm, mla_wq_b, mla_wkv_b, w_branch, w_out,
              ffn2_up, ffn2_down, final_norm):
    s = x.shape[1]
    pos = jnp.arange(s)
    rows = pos // GRID_W
    cols = pos % GRID_W
    cond_lat = jax.nn.silu(c)
    cond_ctx = jax.nn.silu(c_ctx)[None]
    h, hc = x, ctx
    for l in range(DEPTH):
        with_ctx = l < DEPTH - 1
        m = adaln(cond_lat, ada_w[l], ada_b[l])
        mc = adaln(cond_ctx, ada_w[l], ada_b[l])
        h = h + 0.5 * m[2] * swiglu(modulate(rms_norm(h), m[0], m[1]), ffn1_up[l], ffn1_down[l])
        hc = hc + 0.5 * mc[2] * swiglu(modulate(rms_norm(hc), mc[0], mc[1]), ffn1_up[l], ffn1_down[l])
        n = modulate(rms_norm(h), m[3], m[4])
        nc = modulate(rms_norm(hc), mc[3], mc[4])
        hy_p = (hy_conv_w[l], hy_conv_b[l], hy_f_w1[l], hy_f_b[l], hy_f_freq[l], hy_f_w_mid[l], hy_f_w_out[l], hy_skip[l])
        mla_p = (mla_q_norm[l], mla_kv_norm[l], mla_wq_b[l], mla_wkv_b[l])
        mix, mix_c = token_mix(n, nc, rows, cols, w_in[l], (gqa_q_norm[l], gqa_k_norm[l]), hy_p,
                               na_rpb[l], mla_p, w_branch[l], w_out[l], with_ctx)
        h = h + m[5] * mix
        h = h + 0.5 * m[8] * swiglu(modulate(rms_norm(h), m[6], m[7]), ffn2_up[l], ffn2_down[l])
        if with_ctx:
            hc = hc + mc[5] * mix_c
            hc = hc + 0.5 * mc[8] * swiglu(modulate(rms_norm(hc), mc[6], mc[7]), ffn2_up[l], ffn2_down[l])
    return rms_norm(h, final_norm)
```

```python
import math
import numpy as np
import ml_dtypes
import concourse.bass as bass
import concourse.mybir as mybir
from concourse.bass_utils import run_bass_kernel_spmd

F32 = mybir.dt.float32
BF16 = mybir.dt.bfloat16
I32 = mybir.dt.int32
AF = mybir.ActivationFunctionType
ALU = mybir.AluOpType

D = 1024
S = 2048
CT = 256
TT = S + CT
DEPTH = 4
FF = 2816
GW = 64
EPS = 1e-6
TILES = [(0, 512), (512, 512), (1024, 512), (1536, 512), (2048, 256)]
O_GQ, O_GK, O_GV, O_HY, O_NQ, O_NK, O_NV, O_MQ, O_MKV, O_MKR, O_GATE = (
    0, 512, 640, 768, 2304, 2816, 3328, 3840, 4224, 4480, 4512)
NEG = -30000.0
TWO_PI = 2.0 * math.pi


class Trk:
    def __init__(self, nc):
        self.nc = nc
        self.eng = {'pe': nc.tensor, 'act': nc.scalar, 'dve': nc.vector, 'pool': nc.gpsimd, 'sp': nc.sync}
        self.sems = {}
        self.ccnt = {}
        for e in ('pe', 'act', 'dve', 'pool'):
            self.sems['c_' + e] = nc.alloc_semaphore('c_' + e)
            self.ccnt[e] = 0
        self.dring = {}
        self.dcnt = {}
        self.dnext = {}
        for q, n in (('sp', 10), ('pool', 8)):
            self.dring[q] = []
            for i in range(n):
                sid = 'd%s%d' % (q, i)
                self.sems[sid] = nc.alloc_semaphore(sid)
                self.dring[q].append(sid)
            self.dcnt[q] = [0] * n
            self.dnext[q] = 0
        self.waited = {e: {} for e in self.eng}
        self.res = {}
        self.n_ins = 0

    def _rel(self, key):
        grp = self.res.get(key[0])
        if not grp:
            return
        n = len(key)
        for k2, st in grp.items():
            m = min(n, len(k2))
            if k2[:m] == key[:m]:
                yield st

    @staticmethod
    def _k(key):
        return key if isinstance(key, tuple) else (key,)

    def _deps(self, reads, writes):
        deps = {}

        def add(tok):
            if tok is not None and deps.get(tok[0], 0) < tok[1]:
                deps[tok[0]] = tok[1]
        for r in reads:
            k = self._k(r)
            for st in self._rel(k):
                add(st[0])
                if k[0] == 'ps':
                    for sid, v in st[1].items():
                        add((sid, v))
        for w in writes:
            for st in self._rel(self._k(w)):
                add(st[0])
                for sid, v in st[1].items():
                    add((sid, v))
        return deps

    def _wait(self, e, deps):
        wd = self.waited[e]
        for sid, val in deps.items():
            if wd.get(sid, 0) < val:
                self.eng[e].wait_ge(self.sems[sid], val)
                wd[sid] = val
                self.n_ins += 1

    def _commit(self, tok, reads, writes):
        for w in writes:
            k = self._k(w)
            grp = self.res.setdefault(k[0], {})
            for k2 in [k2 for k2 in grp if len(k2) > len(k) and k2[:len(k)] == k]:
                del grp[k2]
            grp[k] = [tok, {}]
        for r in reads:
            k = self._k(r)
            grp = self.res.setdefault(k[0], {})
            st = grp.get(k)
            if st is None:
                st = [None, {}]
                grp[k] = st
            if st[1].get(tok[0], 0) < tok[1]:
                st[1][tok[0]] = tok[1]

    def op(self, e, fn, reads=(), writes=()):
        deps = self._deps(reads, writes)
        if e == 'pe':
            deps.pop('c_pe', None)
        self._wait(e, deps)
        ins = fn(self.eng[e])
        ins.then_inc(self.sems['c_' + e], 1)
        self.ccnt[e] += 1
        self.n_ins += 1
        self._commit(('c_' + e, self.ccnt[e]), reads, writes)

    def mm(self, mms, reads=(), writes=()):
        deps = self._deps(reads, writes)
        deps.pop('c_pe', None)
        self._wait('pe', deps)
        ins = None
        for (o, l, r, st, sp) in mms:
            ins = self.nc.tensor.matmul(o, l, r, start=st, stop=sp)
            self.n_ins += 1
        ins.then_inc(self.sems['c_pe'], 1)
        self.ccnt['pe'] += 1
        self._commit(('c_pe', self.ccnt['pe']), reads, writes)

    def tr(self, out, in_, ident, reads=(), writes=()):
        deps = self._deps(reads, writes)
        deps.pop('c_pe', None)
        self._wait('pe', deps)
        ins = self.nc.tensor.transpose(out, in_, ident)
        ins.then_inc(self.sems['c_pe'], 1)
        self.ccnt['pe'] += 1
        self.n_ins += 1
        self._commit(('c_pe', self.ccnt['pe']), reads, writes)

    def dma(self, q, out, in_, reads=(), writes=()):
        deps = self._deps(reads, writes)
        ring = self.dring[q]
        i = self.dnext[q]
        self.dnext[q] = (i + 1) % len(ring)
        sid = ring[i]
        if self.dcnt[q][i] > 0:
            v = 16 * self.dcnt[q][i]
            if deps.get(sid, 0) < v:
                deps[sid] = v
        self._wait(q, deps)
        self.eng[q].dma_start(out=out, in_=in_).then_inc(self.sems[sid], 16)
        self.dcnt[q][i] += 1
        self.n_ins += 1
        self._commit((sid, 16 * self.dcnt[q][i]), reads, writes)

    def barrier(self, keep=()):
        deps = {}
        for g in list(self.res.keys()):
            if g in keep:
                continue
            for st in self.res[g].values():
                toks = dict(st[1])
                if st[0] is not None:
                    toks[st[0][0]] = max(toks.get(st[0][0], 0), st[0][1])
                for sid, v in toks.items():
                    if deps.get(sid, 0) < v:
                        deps[sid] = v
            del self.res[g]
        for e in self.eng:
            self._wait(e, dict(deps))

    def finish(self):
        deps = {}
        for e, c in self.ccnt.items():
            if c:
                deps['c_' + e] = c
        for q in self.dring:
            for i, c in enumerate(self.dcnt[q]):
                if c:
                    deps[self.dring[q][i]] = 16 * c
        for e in self.eng:
            self._wait(e, dict(deps))


def _bf(a):
    return np.ascontiguousarray(np.asarray(a, dtype=np.float32).astype(ml_dtypes.bfloat16))


def _rope_tables(dim, head_rep, rows_used=None):
    pos = np.arange(S)
    rows = pos // GW
    cols = pos % GW
    half2 = dim // 2
    h = half2 // 2
    freqs = (10000.0 ** (-np.arange(h, dtype=np.float32) / np.float32(h))).astype(np.float32)
    cos = np.ones((dim, TT), np.float32)
    sin = np.zeros((dim, TT), np.float32)
    for d in range(dim):
        blk = d // half2
        i = d % h
        p = (rows if blk == 0 else cols).astype(np.float32)
        ang = (p * freqs[i]).astype(np.float32)
        cos[d, :S] = np.cos(ang)
        sin[d, :S] = np.sin(ang)
    return cos, sin


def _rot_matrix(dim):
    half2 = dim // 2
    h = half2 // 2
    R = np.zeros((dim, dim), np.float32)
    for b in (0, half2):
        for i in range(h):
            R[b + i + h, b + i] = -1.0
            R[b + i, b + i + h] = 1.0
    return R


def _hy_consts(L):
    N = 2 * L
    t = np.linspace(0.0, 1.0, L, dtype=np.float32)[:, None]
    w = (2.0 * math.pi / L) * np.arange(L, dtype=np.float32)[:, None]
    f = np.linspace(1e-4, 15, 16, dtype=np.float32)[None, :]
    z = np.concatenate([t, np.cos(f * w), -np.sin(f * w)], axis=-1).astype(np.float32)
    mn = math.log(1e-2) / 1.5
    mx = math.log(1e-2) / 0.3
    deltas = np.abs(np.linspace(mn, mx, 512, dtype=np.float32))
    dec = np.exp(-t * deltas).astype(np.float32)
    decb = dec.copy()
    decb[0] = 0.0
    nt = L // 128

    def tm(a):
        return np.ascontiguousarray(a.reshape(nt, 128, -1).transpose(1, 0, 2))
    tt = np.arange(L, dtype=np.float64)[:, None]
    ff = np.arange(L, dtype=np.float64)[None, :]
    ang = 2.0 * math.pi * tt * ff / N
    CF = np.cos(ang)
    SF = -np.sin(ang)
    SF[:, 0] = np.cos(math.pi * tt[:, 0])
    IC = (2.0 / N) * np.cos(ang.T)
    IC[0, :] = 1.0 / N
    IS = -(2.0 / N) * np.sin(ang.T)
    IS[0, :] = np.cos(math.pi * tt[:, 0]) / N

    def fwd(a):
        return _bf(a.reshape(nt, 128, nt, 128).transpose(2, 1, 0, 3))

    def inv(a):
        return _bf(a.reshape(nt, 128, L).transpose(1, 0, 2))
    return dict(zT=_bf(z.T), decF=tm(dec), decB=tm(decb), CF=fwd(CF), SF=fwd(SF), IC=inv(IC), IS=inv(IS))


def _na_table(rpb):
    qc = np.arange(GW)[:, None]
    kc = np.arange(GW)[None, :]
    col_start = np.clip(qc - 8, 0, GW - 16)
    col_ok = (kc >= col_start) & (kc < col_start + 16)
    col_idx = np.clip(kc - qc + 15, 0, 30)
    toe = rpb[:, :, :, col_idx]
    toe = np.where(col_ok, toe, np.float32(NEG)).astype(np.float32)
    toe = toe.transpose(0, 1, 2, 4, 3)
    pair = np.stack([toe[:, :, 0:14], toe[:, :, 1:15]], axis=3)
    pair = pair.reshape(pair.shape[0], 8, 14, 128, GW).transpose(0, 1, 3, 2, 4)
    pair = pair.reshape(pair.shape[0], 8, 128, 7, 2, GW).transpose(0, 1, 2, 4, 3, 5)
    return np.ascontiguousarray(pair)


_CONST_CACHE = {}


def _consts():
    if _CONST_CACHE:
        return _CONST_CACHE
    c = {}
    c['ident'] = np.eye(128, dtype=np.float32)
    c['identb'] = _bf(np.eye(128))
    c['onesb'] = _bf(np.ones((128, 128)))
    bd = np.zeros((128, 128), np.float32)
    bd[:64, :64] = 1
    bd[64:, 64:] = 1
    c['bdb'] = _bf(bd)
    R64 = _rot_matrix(64)
    Rg = np.zeros((128, 128), np.float32)
    Rg[:64, :64] = R64
    Rg[64:, 64:] = R64
    c['Rg'] = _bf(Rg)
    Rm = np.zeros((128, 128), np.float32)
    Rm[64:96, 64:96] = _rot_matrix(32)
    c['Rm'] = _bf(Rm)
    cg, sg = _rope_tables(64, 2)
    c['cosG'] = np.ascontiguousarray(np.concatenate([cg, cg], 0))
    c['sinG'] = np.ascontiguousarray(np.concatenate([sg, sg], 0))
    cm, sm = _rope_tables(32, 1)
    cosM = np.ones((128, TT), np.float32)
    sinM = np.zeros((128, TT), np.float32)
    cosM[64:96] = cm
    sinM[64:96] = sm
    c['cosM'] = cosM
    c['sinM'] = sinM
    for L in (S, CT):
        for k, v in _hy_consts(L).items():
            c['%s%d' % (k, L)] = v
    _CONST_CACHE.update(c)
    return c


def _layout_inputs(inp, b, NL):
    f = lambda a: np.ascontiguousarray(np.asarray(a, dtype=np.float32))
    m = {}
    m['x'] = f(inp['x'][b])
    m['ctx'] = f(inp['ctx'][b])
    m['cond'] = f(np.stack([np.asarray(inp['c'][b]).reshape(8, 128).T,
                            np.asarray(inp['c_ctx']).reshape(8, 128).T], axis=-1))
    for k in ('ada_w', 'ffn1_up', 'ffn1_down', 'w_in', 'mla_wq_b', 'mla_wkv_b', 'w_branch', 'w_out',
              'ffn2_up', 'ffn2_down', 'hy_f_w1', 'hy_f_w_out'):
        m[k] = f(inp[k][:NL])
    m['ada_b'] = f(np.asarray(inp['ada_b'][:NL]).reshape(NL, 72, 128).transpose(0, 2, 1))
    m['gq'] = f(np.tile(np.asarray(inp['gqa_q_norm'][:NL]), (1, 2))[:, :, None])
    m['gk'] = f(np.tile(np.asarray(inp['gqa_k_norm'][:NL]), (1, 2))[:, :, None])
    m['hy_cw'] = f(np.asarray(inp['hy_conv_w'][:NL]).reshape(NL, 3, 12, 128).transpose(0, 3, 2, 1))
    m['hy_cb'] = f(np.asarray(inp['hy_conv_b'][:NL]).reshape(NL, 12, 128).transpose(0, 2, 1))
    m['hy_skip'] = f(np.asarray(inp['hy_skip'][:NL]).reshape(NL, 4, 128).transpose(0, 2, 1))
    m['hy_fb'] = f(np.asarray(inp['hy_f_b'][:NL]).transpose(0, 2, 1))
    m['hy_ff'] = f(np.asarray(inp['hy_f_freq'][:NL]).transpose(0, 2, 1))
    m['hy_wmid'] = f(np.asarray(inp['hy_f_w_mid'][:NL]).transpose(0, 2, 1, 3))
    m['na_tab'] = _na_table(np.asarray(inp['na_rpb'][:NL], dtype=np.float32))
    m['mla_qn'] = f(np.asarray(inp['mla_q_norm'][:NL]).reshape(NL, 3, 128).transpose(0, 2, 1))
    m['mla_kvn'] = f(np.asarray(inp['mla_kv_norm'][:NL]).reshape(NL, 2, 128).transpose(0, 2, 1))
    m['fnorm'] = f(np.asarray(inp['final_norm']).reshape(8, 128).T)
    m.update(_consts())
    return m


def build(NL, stop=None, dump=()):
    nc = bass.Bass("TRN2", target_bir_lowering=False)
    T = Trk(nc)

    def din(name, shape, dt=F32):
        return nc.dram_tensor(name, list(shape), dt, kind="ExternalInput").ap()

    def dscr(name, shape, dt):
        return nc.dram_tensor(name, list(shape), dt, kind="Internal").ap()

    x_d = din('x', [S, D]); ctx_d = din('ctx', [CT, D]); cond_d = din('cond', [128, 8, 2])
    ada_w = din('ada_w', [NL, D, 9 * D]); ada_b = din('ada_b', [NL, 128, 72])
    ffn1_up = din('ffn1_up', [NL, D, 2 * FF]); ffn1_down = din('ffn1_down', [NL, FF, D])
    ffn2_up = din('ffn2_up', [NL, D, 2 * FF]); ffn2_down = din('ffn2_down', [NL, FF, D])
    w_in = din('w_in', [NL, D, 8608])
    gq_d = din('gq', [NL, 128, 1]); gk_d = din('gk', [NL, 128, 1])
    hy_cw = din('hy_cw', [NL, 128, 12, 3]); hy_cb = din('hy_cb', [NL, 128, 12]); hy_skip = din('hy_skip', [NL, 128, 4])
    hy_fw1 = din('hy_f_w1', [NL, 33, 64]); hy_fb = din('hy_fb', [NL, 64, 3]); hy_ff = din('hy_ff', [NL, 64, 3])
    hy_wmid = din('hy_wmid', [NL, 64, 2, 64]); hy_wout = din('hy_f_w_out', [NL, 64, 1024])
    na_tab = din('na_tab', [NL, 8, 128, 2, 7, 64])
    mla_qn = din('mla_qn', [NL, 128, 3]); mla_kvn = din('mla_kvn', [NL, 128, 2])
    mla_wq = din('mla_wq_b', [NL, 384, 768]); mla_wkv = din('mla_wkv_b', [NL, 256, 1024])
    w_branch = din('w_branch', [NL, 4, 512, D]); w_out = din('w_out', [NL, D, D])
    fnorm_d = din('fnorm', [128, 8])
    cst = {}
    for nm, shp, dt in (('ident', [128, 128], F32), ('identb', [128, 128], BF16), ('onesb', [128, 128], BF16),
                        ('bdb', [128, 128], BF16), ('Rg', [128, 128], BF16), ('Rm', [128, 128], BF16),
                        ('cosG', [128, TT], F32), ('sinG', [128, TT], F32), ('cosM', [128, TT], F32),
                        ('sinM', [128, TT], F32)):
        cst[nm] = din(nm, shp, dt)
    for L in (S, CT):
        nt = L // 128
        for nm, shp, dt in (('zT', [33, L], BF16), ('decF', [128, nt, 512], F32), ('decB', [128, nt, 512], F32),
                            ('CF', [nt, 128, nt, 128], BF16), ('SF', [nt, 128, nt, 128], BF16),
                            ('IC', [128, nt, L], BF16), ('IS', [128, nt, L], BF16)):
            cst['%s%d' % (nm, L)] = din('%s%d' % (nm, L), shp, dt)
    out_d = nc.dram_tensor('out', [S, D], F32, kind="ExternalOutput").ap()
    dbg_d = {}
    h_spill = dscr('h_spill', [128, 8, TT], F32)
    o_scr = dscr('o_scr', [4, 128, 4, TT], BF16)
    hy_x0 = dscr('hy_x0', [4, 128, TT], F32)
    hy_vx = dscr('hy_vx', [4, 128, TT], F32)

    total = nc.sbuf_bytes_remaining - 3072
    total -= total % 64
    BIG = nc.alloc_sbuf_tensor('BIG', [128, total // 2], BF16)
    state = {'off': 0}

    def alloc(shape, dt):
        n = 1
        for v in shape:
            n *= v
        nb = n * (4 if dt in (F32, I32) else 2)
        nb = (nb + 31) // 32 * 32
        o = state['off']
        assert o + nb <= total, ('SBUF overflow', o, nb, total)
        state['off'] = o + nb
        a = BIG[:, o // 2:(o + nb) // 2]
        if dt != BF16:
            a = a.bitcast(dt)
        a = a[:, 0:n]
        if len(shape) == 2:
            a = a.rearrange("p (a b) -> p a b", a=shape[0], b=shape[1])
        elif len(shape) == 3:
            a = a.rearrange("p (a b c) -> p a b c", a=shape[0], b=shape[1], c=shape[2])
        elif len(shape) == 4:
            a = a.rearrange("p (a b c d) -> p a b c d", a=shape[0], b=shape[1], c=shape[2], d=shape[3])
        return a

    ident = alloc([128], F32); identb = alloc([128], BF16); onesb = alloc([128], BF16)
    bdb = alloc([128], BF16); Rg = alloc([128], BF16); Rm = alloc([128], BF16)
    condT = alloc([8, 2], BF16); condf = alloc([16], F32)
    mod = alloc([2, 72], F32); adab = alloc([72], F32)
    gq = alloc([1], F32); gk = alloc([1], F32)
    hycw = alloc([12, 3], F32); hycb = alloc([12], F32); hysk = alloc([4], F32)
    hyfb = alloc([3], F32); hyff = alloc([3], F32); hyfbf = alloc([3], F32)
    mqn = alloc([3], F32); mkvn = alloc([2], F32)
    hinv = alloc([4], F32); hinvc = alloc([4], F32)
    nT = alloc([8, TT], BF16)
    NSLOT = 6
    slots = [alloc([4096], BF16) for _ in range(NSLOT)]
    hT_off = state['off']
    hT = alloc([8, TT], F32)
    small_off = state['off']
    sst = {'i': 0}

    def slot():
        i = sst['i']
        sst['i'] = (i + 1) % NSLOT
        return i

    def sview(i, shape, n=None):
        a = slots[i]
        tot = 1
        for v in shape:
            tot *= v
        assert tot <= 4096
        a = a[:, 0:tot]
        if len(shape) == 2:
            a = a.rearrange("p (a b) -> p a b", a=shape[0], b=shape[1])
        elif len(shape) == 3:
            a = a.rearrange("p (a b c) -> p a b c", a=shape[0], b=shape[1], c=shape[2])
        elif len(shape) == 4:
            a = a.rearrange("p (a b c d) -> p a b c d", a=shape[0], b=shape[1], c=shape[2], d=shape[3])
        return a

    PP = [nc.alloc_psum_tensor('pp%d' % i, [128, 1024], F32) for i in range(4)]

    def ps(b):
        return PP[b // 2][:, (b % 2) * 512:(b % 2) * 512 + 512]

    def pk(b):
        return ('ps', b)

    def arena_small():
        state['off'] = small_off

    def arena_big():
        state['off'] = hT_off

    def wload(si, view, src):
        T.dma('pool', view, src, writes=[('slot', si)])

    def kc_rows(w2d, c0, cn):
        return w2d.rearrange("(kc p) n -> p kc n", p=128)[:, :, c0:c0 + cn]

    for nm, sb in (('ident', ident), ('identb', identb), ('onesb', onesb), ('bdb', bdb), ('Rg', Rg), ('Rm', Rm)):
        T.dma('sp', sb, cst[nm], writes=[nm])
    T.dma('sp', condf, cond_d.rearrange("p a b -> p (a b)"), writes=['condf'])
    T.op('act', lambda e: e.activation(out=condT.rearrange("p a b -> p (a b)"), in_=condf, func=AF.Silu),
         reads=['condf'], writes=['condT'])
    arena_small()
    stage = [alloc([D], F32) for _ in range(2)]
    for tt in range(18):
        src = x_d[tt * 128:(tt + 1) * 128, :] if tt < 16 else ctx_d[(tt - 16) * 128:(tt - 15) * 128, :]
        st = stage[tt % 2]
        T.dma('sp', st, src, writes=[('stage', tt % 2)])
        for half in range(2):
            b = (tt * 2 + half) % 4
            for j in range(4):
                kc = half * 4 + j
                T.tr(ps(b)[:, j * 128:(j + 1) * 128], st[:, kc * 128:(kc + 1) * 128], ident,
                     reads=[('stage', tt % 2), 'ident'], writes=[pk(b)])
            ti = min(tt // 4, 4)
            dst = hT[:, half * 4:half * 4 + 4, tt * 128:(tt + 1) * 128]
            srcp = ps(b).rearrange("p (a b) -> p a b", a=4, b=128)
            eng = 'dve' if half == 0 else 'act'
            if eng == 'dve':
                T.op('dve', lambda e: e.tensor_copy(out=dst, in_=srcp), reads=[pk(b)], writes=[('hT', ti, 'i%d' % tt, half)])
            else:
                T.op('act', lambda e: e.activation(out=dst, in_=srcp, func=AF.Copy), reads=[pk(b)],
                     writes=[('hT', ti, 'i%d' % tt, half)])
    T.barrier(keep=('ident', 'identb', 'onesb', 'bdb', 'Rg', 'Rm', 'condT', 'hT'))

    KEEP = ('ident', 'identb', 'onesb', 'bdb', 'Rg', 'Rm', 'condT', 'hT', 'nT', 'mod', 'slot', 'o_scr', 'h_spill')

    def adaln(li):
        T.dma('sp', adab, ada_b[li], writes=['adab'])
        pm = ps(7)
        for j in range(18):
            si = slot()
            wv = sview(si, [8, 512])
            wload(si, wv, kc_rows(ada_w[li], j * 512, 512))
            for mm_ in range(4):
                col = (j * 4 + mm_) * 2
                T.mm([(pm[:, col:col + 2], wv[:, kc, mm_ * 128:(mm_ + 1) * 128], condT[:, kc, :], kc == 0, kc == 7)
                      for kc in range(8)], reads=[('slot', si), 'condT'], writes=[pk(7)])
        pmv = pm[:, 0:144].rearrange("p (j two) -> p j two", two=2)
        for lc in range(2):
            T.op('dve', lambda e, lc=lc: e.tensor_tensor(out=mod[:, lc, :], in0=pmv[:, :, lc], in1=adab, op=ALU.add),
                 reads=[pk(7), 'adab'], writes=[('mod', lc)])
        for i in (1, 4, 7):
            T.op('dve', lambda e, i=i: e.tensor_scalar_add(out=mod[:, :, i * 8:(i + 1) * 8], in0=mod[:, :, i * 8:(i + 1) * 8], scalar1=1.0),
                 reads=['mod'], writes=['mod'])
        for i in (2, 8):
            T.op('dve', lambda e, i=i: e.tensor_scalar_mul(out=mod[:, :, i * 8:(i + 1) * 8], in0=mod[:, :, i * 8:(i + 1) * 8], scalar1=0.5),
                 reads=['mod'], writes=['mod'])

    def mcol(lc, i, kc):
        return mod[:, lc, i * 8 + kc:i * 8 + kc + 1]

    def norm_mod(which, tiles):
        arena_small()
        sq = alloc([8, 512], BF16)
        rs = [alloc([512], F32) for _ in range(2)]
        tmp = [alloc([512], F32) for _ in range(2)]
        for ti in tiles:
            t0, tn = TILES[ti]
            lc = 0 if ti < 4 else 1
            r = rs[ti % 2]
            T.op('act', lambda e: e.activation(out=sq[:, :, :tn], in_=hT[:, :, t0:t0 + tn], func=AF.Square),
                 reads=[('hT', ti)], writes=['sq'])
            T.mm([(ps(6)[:, :tn], onesb, sq[:, kc, :tn], kc == 0, kc == 7) for kc in range(8)],
                 reads=['sq', 'onesb'], writes=[pk(6)])
            T.op('act', lambda e: e.activation(out=r[:, :tn], in_=ps(6)[:, :tn], func=AF.Sqrt, bias=EPS, scale=1.0 / D),
                 reads=[pk(6)], writes=[('rs', ti % 2)])
            T.op('dve', lambda e: e.reciprocal(out=r[:, :tn], in_=r[:, :tn]), reads=[('rs', ti % 2)], writes=[('rs', ti % 2)])
            for kc in range(8):
                tp = tmp[kc % 2]
                T.op('dve', lambda e, kc=kc, tp=tp: e.scalar_tensor_tensor(
                    out=tp[:, :tn], in0=hT[:, kc, t0:t0 + tn], scalar=mcol(lc, 3 * which + 1, kc), in1=r[:, :tn],
                    op0=ALU.mult, op1=ALU.mult), reads=[('hT', ti), ('rs', ti % 2), 'mod'], writes=[('tmp', kc % 2)])
                T.op('act', lambda e, kc=kc, tp=tp: e.activation(
                    out=nT[:, kc, t0:t0 + tn], in_=tp[:, :tn], func=AF.Identity, bias=mcol(lc, 3 * which, kc), scale=1.0),
                    reads=[('tmp', kc % 2), 'mod'], writes=[('nT', ti, kc)])

    def ffn(li, w_up, w_down, gidx, tiles):
        arena_small()
        uT = alloc([4, TT], BF16)
        sa = [alloc([512], F32) for _ in range(2)]
        groups = [(0, 4), (4, 4), (8, 4), (12, 4), (16, 4), (20, 2)]

        def load(g):
            c0, gn = groups[g]
            s3 = (slot(), slot(), slot())
            va = sview(s3[0], [8, gn * 128]); vg = sview(s3[1], [8, gn * 128]); vd = sview(s3[2], [gn, D])
            wload(s3[0], va, kc_rows(w_up[li], c0 * 128, gn * 128))
            wload(s3[1], vg, kc_rows(w_up[li], FF + c0 * 128, gn * 128))
            wload(s3[2], vd, w_down[li].rearrange("(c p) n -> p c n", p=128)[:, c0:c0 + gn, :])
            return s3, (va, vg, vd)
        nxt = load(0)
        cnt = 0
        for g, (c0, gn) in enumerate(groups):
            s3, (va, vg, vd) = nxt
            if g + 1 < len(groups):
                nxt = load(g + 1)
            for c in range(gn):
                for ti in tiles:
                    t0, tn = TILES[ti]
                    ba, bg = (cnt % 2), 2 + (cnt % 2)
                    T.mm([(ps(ba)[:, :tn], va[:, kc, c * 128:(c + 1) * 128], nT[:, kc, t0:t0 + tn], kc == 0, kc == 7)
                          for kc in range(8)], reads=[('slot', s3[0]), ('nT', ti)], writes=[pk(ba)])
                    T.mm([(ps(bg)[:, :tn], vg[:, kc, c * 128:(c + 1) * 128], nT[:, kc, t0:t0 + tn], kc == 0, kc == 7)
                          for kc in range(8)], reads=[('slot', s3[1]), ('nT', ti)], writes=[pk(bg)])
                    s_ = sa[cnt % 2]
                    T.op('act', lambda e, s_=s_, ba=ba: e.activation(out=s_[:, :tn], in_=ps(ba)[:, :tn], func=AF.Silu),
                         reads=[pk(ba)], writes=[('sa', cnt % 2)])
                    T.op('dve', lambda e, s_=s_, bg=bg, c=c: e.tensor_tensor(out=uT[:, c, t0:t0 + tn], in0=s_[:, :tn], in1=ps(bg)[:, :tn], op=ALU.mult),
                         reads=[('sa', cnt % 2), pk(bg)], writes=[('uT', ti, c)])
                    cnt += 1
            for m in range(8):
                for ti in tiles:
                    t0, tn = TILES[ti]
                    lc = 0 if ti < 4 else 1
                    bo = 4 + (cnt % 2)
                    cnt += 1
                    T.mm([(ps(bo)[:, :tn], vd[:, c, m * 128:(m + 1) * 128], uT[:, c, t0:t0 + tn], c == 0, c == gn - 1)
                          for c in range(gn)], reads=[('slot', s3[2]), ('uT', ti)], writes=[pk(bo)])
                    T.op('dve', lambda e, bo=bo, m=m: e.scalar_tensor_tensor(
                        out=hT[:, m, t0:t0 + tn], in0=ps(bo)[:, :tn], scalar=mcol(lc, gidx, m), in1=hT[:, m, t0:t0 + tn],
                        op0=ALU.mult, op1=ALU.add), reads=[pk(bo), 'mod', ('hT', ti, m)], writes=[('hT', ti, m)])

    def attention(kfun, q_ap, vfun, chunks, tn, scale, par, out_ap, rd, wr, bufs):
        pT, dt, cnt = bufs
        pob = 3 + (cnt['po'] % 2)
        cnt['po'] += 1
        n = len(chunks)
        base = cnt['s']
        cnt['s'] += n

        def qk(i):
            sb = (base + i) % 3
            T.mm([(ps(sb)[:, :tn], kfun(chunks[i]), q_ap, True, True)], reads=rd, writes=[pk(sb)])
        qk(0)
        if n > 1:
            qk(1)
        for i in range(n):
            if i + 2 < n:
                qk(i + 2)
            sb = (base + i) % 3
            pt = pT[sb]
            T.op('act', lambda e, pt=pt, sb=sb: e.activation(out=pt[:, :tn], in_=ps(sb)[:, :tn], func=AF.Exp, scale=scale),
                 reads=[pk(sb)], writes=[('pT', sb)])
            T.mm([(ps(pob)[:, :tn], vfun(chunks[i]), pt[:, :tn], i == 0, i == n - 1)],
                 reads=[('pT', sb)] + rd, writes=[pk(pob)])
        o0, d0 = 64 * par, 64 * (1 - par)
        d_ = dt[cnt['po'] % 2]
        dk = ('dt', cnt['po'] % 2)
        T.op('dve', lambda e: e.tensor_copy(out=d_[o0:o0 + 64, :tn], in_=ps(pob)[d0:d0 + 64, :tn]), reads=[pk(pob)], writes=[dk])
        T.op('dve', lambda e: e.reciprocal(out=d_[o0:o0 + 64, :tn], in_=d_[o0:o0 + 64, :tn]), reads=[dk], writes=[dk])
        T.op('dve', lambda e: e.tensor_tensor(out=out_ap, in0=ps(pob)[o0:o0 + 64, :tn], in1=d_[o0:o0 + 64, :tn], op=ALU.mult),
             reads=[pk(pob), dk], writes=wr)

    def norm_rope(psb, tn, t0, dst, gain, cosT, sinT, Rmat, red, inv_n, tmps, par, rd, wr, np_=128):
        sqb, rs, xn, t1, t2 = [t[par] for t in tmps]
        k = lambda s: (s, par)
        T.op('act', lambda e: e.activation(out=sqb[:np_, :tn], in_=psb, func=AF.Square), reads=rd, writes=[k('sqb')])
        T.mm([(ps(5)[:np_, :tn], red[:np_, :np_], sqb[:np_, :tn], True, True)], reads=[k('sqb'), 'bdb', 'onesb'], writes=[pk(5)])
        T.op('act', lambda e: e.activation(out=rs[:np_, :tn], in_=ps(5)[:np_, :tn], func=AF.Sqrt, bias=EPS, scale=inv_n),
             reads=[pk(5)], writes=[k('rs')])
        T.op('dve', lambda e: e.reciprocal(out=rs[:np_, :tn], in_=rs[:np_, :tn]), reads=[k('rs')], writes=[k('rs')])
        T.op('dve', lambda e: e.scalar_tensor_tensor(out=xn[:np_, :tn], in0=psb, scalar=gain, in1=rs[:np_, :tn],
                                                       op0=ALU.mult, op1=ALU.mult), reads=rd + [k('rs')], writes=[k('xn')])
        rope(xn, np_, tn, t0, dst, cosT, sinT, Rmat, (t1, t2), par, [k('xn')], wr)

    def rope(xn, np_, tn, t0, dst, cosT, sinT, Rmat, t12, par, rd, wr, p0=0):
        t1, t2 = t12
        k = lambda s: (s, par)
        T.mm([(ps(6)[:np_, :tn], Rmat[p0:np_, :np_], xn[p0:np_, :tn], True, True)], reads=rd + ['Rg', 'Rm'], writes=[pk(6)])
        T.op('dve', lambda e: e.tensor_tensor(out=t1[p0:np_, :tn], in0=xn[p0:np_, :tn], in1=cosT[p0:np_, t0:t0 + tn], op=ALU.mult),
             reads=rd + ['cos'], writes=[k('t1')])
        T.op('dve', lambda e: e.tensor_tensor(out=t2[p0:np_, :tn], in0=ps(6)[p0:np_, :tn], in1=sinT[p0:np_, t0:t0 + tn], op=ALU.mult),
             reads=[pk(6), 'sin'], writes=[k('t2')])
        T.op('pool', lambda e: e.tensor_tensor(out=dst, in0=t1[p0:np_, :tn], in1=t2[p0:np_, :tn], op=ALU.add),
             reads=[k('t1'), k('t2')], writes=wr)

    def rope_tmps():
        return ([alloc([512], BF16) for _ in range(2)], [alloc([512], F32) for _ in range(2)],
                [alloc([512], BF16) for _ in range(2)], [alloc([512], F32) for _ in range(2)],
                [alloc([512], F32) for _ in range(2)])

    def gqa(li, with_ctx, stage=9):
        arena_big()
        T.dma('sp', gq, gq_d[li], writes=['gq']); T.dma('sp', gk, gk_d[li], writes=['gk'])
        cosT = alloc([TT], F32); sinT = alloc([TT], F32)
        T.dma('sp', cosT, cst['cosG'], writes=['cos']); T.dma('sp', sinT, cst['sinG'], writes=['sin'])
        qT = alloc([4, TT], BF16); kT = alloc([2, TT], BF16); va = alloc([18, 2, 2, 128], BF16)
        oT = alloc([4, TT], BF16)
        tmps = rope_tmps()
        pT = [alloc([512], BF16) for _ in range(3)]; dt = [alloc([512], F32) for _ in range(2)]
        T.op('pool', lambda e: e.memset(va.rearrange("p a b c d -> p (a b c d)"), 1.0), writes=['va'])
        sq_ = slot(); wq = sview(sq_, [8, 512]); wload(sq_, wq, kc_rows(w_in[li], O_GQ, 512))
        sk_ = slot(); wkraw = sview(sk_, [8, 128]); wload(sk_, wkraw, kc_rows(w_in[li], O_GK, 128))
        wk = alloc([8, 2, 2, 64], BF16)
        for dup in range(2):
            T.op('pool', lambda e: e.tensor_copy(out=wk[:, :, :, dup, :], in_=wkraw.rearrange("p k (g d) -> p k g d", g=2, d=64)),
                 reads=[('slot', sk_)], writes=[('wkd', dup)])
        sv_ = slot(); wv = sview(sv_, [8, 128]); wload(sv_, wv, kc_rows(w_in[li], O_GV, 128))
        cnt = 0
        for qc in range(4):
            for ti in range(5):
                t0, tn = TILES[ti]
                b = cnt % 2; cnt += 1
                T.mm([(ps(b)[:, :tn], wq[:, kc, qc * 128:(qc + 1) * 128], nT[:, kc, t0:t0 + tn], kc == 0, kc == 7) for kc in range(8)],
                     reads=[('slot', sq_), ('nT', ti)], writes=[pk(b)])
                norm_rope(ps(b)[:, :tn], tn, t0, qT[:, qc, t0:t0 + tn], gq, cosT, sinT, Rg, bdb, 1.0 / 64, tmps, b,
                          [pk(b), 'gq'], [('qT', qc, ti)])
        for g in range(2 if stage >= 2 else 0):
            for ti in range(5):
                t0, tn = TILES[ti]
                b = cnt % 2; cnt += 1
                T.mm([(ps(b)[:, :tn], wk[:, kc, g].rearrange("p a b -> p (a b)"), nT[:, kc, t0:t0 + tn], kc == 0, kc == 7) for kc in range(8)],
                     reads=['wkd', ('nT', ti)], writes=[pk(b)])
                norm_rope(ps(b)[:, :tn], tn, t0, kT[:, g, t0:t0 + tn], gk, cosT, sinT, Rg, bdb, 1.0 / 64, tmps, b,
                          [pk(b), 'gk'], [('kT', g, ti)])
        for c in range(18 if stage >= 3 else 0):
            b = cnt % 2; cnt += 1
            T.mm([(ps(b)[:, 0:128], nT[:, kc, c * 128:(c + 1) * 128], wv[:, kc, :], kc == 0, kc == 7) for kc in range(8)],
                 reads=[('slot', sv_), ('nT', c // 4)], writes=[pk(b)])
            pv = ps(b)[:, 0:128].rearrange("p (g d) -> p g d", g=2, d=64)
            T.op('dve', lambda e, c=c, pv=pv: e.tensor_copy(out=va[:, c, :, 0, 0:64], in_=pv), reads=[pk(b)], writes=[('va', c, 0)])
            T.op('act', lambda e, c=c, pv=pv: e.activation(out=va[:, c, :, 1, 64:128], in_=pv, func=AF.Copy), reads=[pk(b)], writes=[('va', c, 1)])
        acnt = {'po': 0, 's': 0}
        for h in range(8 if stage >= 5 else (1 if stage >= 4 else 0)):
            g, qc, par = h // 4, h // 2, h % 2
            off = 64 * par
            jobs = [(ti, list(range(18))) for ti in range(4)]
            if with_ctx:
                jobs.append((4, [16, 17]))
            for ti, chunks in jobs:
                t0, tn = TILES[ti]
                attention(lambda c: kT[off:off + 64, g, c * 128:(c + 1) * 128], qT[off:off + 64, qc, t0:t0 + tn],
                          lambda c: va[:, c, g, par, :], chunks, tn, 0.125, par, oT[off:off + 64, qc, t0:t0 + tn],
                          [('qT', qc, ti), ('kT', g), 'va'], [('oT', qc, ti, par)], (pT, dt, acnt))
        if not with_ctx:
            T.op('pool', lambda e: e.memset(oT[:, :, S:TT], 0.0), writes=[('oT', 'ctxz')])
        T.dma('sp', o_scr[0], oT, reads=['oT'], writes=[('o_scr', 0)])

    def mla(li, with_ctx, stage=9):
        arena_big()
        T.dma('sp', mqn, mla_qn[li], writes=['mqn']); T.dma('sp', mkvn, mla_kvn[li], writes=['mkvn'])
        cosT = alloc([TT], F32); sinT = alloc([TT], F32)
        T.dma('sp', cosT, cst['cosM'], writes=['cos']); T.dma('sp', sinT, cst['sinM'], writes=['sin'])
        qaT = alloc([3, TT], BF16); kvT = alloc([2, TT], BF16); krT = alloc([TT], BF16)
        QhT = alloc([2, TT], BF16); KhT = alloc([2, TT], BF16); va = alloc([18, 2, 128], BF16); oT = alloc([TT], BF16)
        sq3 = alloc([3, 512], BF16)
        rs = [alloc([512], F32) for _ in range(2)]
        xq = [alloc([512], BF16) for _ in range(2)]
        t1 = [alloc([512], F32) for _ in range(2)]; t2 = [alloc([512], F32) for _ in range(2)]
        pT = [alloc([512], BF16) for _ in range(3)]; dt = [alloc([512], F32) for _ in range(2)]
        if not with_ctx:
            T.op('pool', lambda e: e.memset(oT[:, S:TT], 0.0), writes=[('oT', 'ctxz')])
        for (nm, col0, nch, gain, dstT, invn) in (('qa', O_MQ, 3, mqn, qaT, 1.0 / 384), ('kva', O_MKV, 2, mkvn, kvT, 1.0 / 256)):
            si = slot(); w = sview(si, [8, nch * 128]); wload(si, w, kc_rows(w_in[li], col0, nch * 128))
            for ti in range(5):
                t0, tn = TILES[ti]
                for c in range(nch):
                    T.mm([(ps(c)[:, :tn], w[:, kc, c * 128:(c + 1) * 128], nT[:, kc, t0:t0 + tn], kc == 0, kc == 7) for kc in range(8)],
                         reads=[('slot', si), ('nT', ti)], writes=[pk(c)])
                    T.op('act', lambda e, c=c: e.activation(out=sq3[:, c, :tn], in_=ps(c)[:, :tn], func=AF.Square),
                         reads=[pk(c)], writes=[('sq3', c)])
                T.mm([(ps(5)[:, :tn], onesb, sq3[:, c, :tn], c == 0, c == nch - 1) for c in range(nch)],
                     reads=['sq3', 'onesb'], writes=[pk(5)])
                r = rs[ti % 2]
                T.op('act', lambda e: e.activation(out=r[:, :tn], in_=ps(5)[:, :tn], func=AF.Sqrt, bias=EPS, scale=invn),
                     reads=[pk(5)], writes=[('rs', ti % 2)])
                T.op('dve', lambda e: e.reciprocal(out=r[:, :tn], in_=r[:, :tn]), reads=[('rs', ti % 2)], writes=[('rs', ti % 2)])
                for c in range(nch):
                    T.op('dve', lambda e, c=c: e.scalar_tensor_tensor(out=dstT[:, c, t0:t0 + tn], in0=ps(c)[:, :tn], scalar=gain[:, c:c + 1],
                                                                       in1=r[:, :tn], op0=ALU.mult, op1=ALU.mult),
                         reads=[pk(c), ('rs', ti % 2), nm + 'g', 'mqn', 'mkvn'], writes=[(nm + 'T', ti, c)])
        si = slot(); wkr = sview(si, [8, 128]); wload(si, wkr, kc_rows(w_in[li], O_MKR - 96, 128))
        for ti in range(5):
            t0, tn = TILES[ti]
            b = ti % 2
            T.mm([(ps(b)[:96, :tn], wkr[:, kc, 32:128], nT[:, kc, t0:t0 + tn], kc == 0, kc == 7) for kc in range(8)],
                 reads=[('slot', si), ('nT', ti)], writes=[pk(b)])
            T.op('act', lambda e: e.activation(out=xq[b][64:96, :tn], in_=ps(b)[64:96, :tn], func=AF.Copy), reads=[pk(b)], writes=[('xq', b)])
            rope(xq[b], 96, tn, t0, krT[64:96, t0:t0 + tn], cosT, sinT, Rm, (t1[b], t2[b]), b, [('xq', b)], [('krT', ti)], p0=64)
        if stage < 2:
            T.dma('sp', o_scr[3][:, 0, :], krT, reads=['krT'], writes=[('o_scr', 3)])
            T.dma('sp', o_scr[3][:, 1, :], qaT[:, 0, :], reads=['qaT'], writes=[('o_scr', 3, 1)])
            T.dma('sp', o_scr[3][:, 2, :], kvT[:, 0, :], reads=['kvaT'], writes=[('o_scr', 3, 2)])
            return
        acnt = {'po': 0, 's': 0}
        cnt = 0
        qtiles = range(5) if with_ctx else range(4)
        for pp in range(4 if stage >= 3 else 1):
            sq_ = slot(); wq = sview(sq_, [3, 192]); wload(sq_, wq, mla_wq[li].rearrange("(kc p) n -> p kc n", p=128)[:, :, 192 * pp:192 * pp + 192])
            sk_ = slot(); wkv = sview(sk_, [2, 256]); wload(sk_, wkv, mla_wkv[li].rearrange("(kc p) n -> p kc n", p=128)[:, :, 256 * pp:256 * pp + 256])
            T.op('pool', lambda e: e.memset(va.rearrange("p a b c -> p (a b c)"), 1.0), reads=[], writes=['va'])
            for hh in range(2):
                for ti in range(5):
                    t0, tn = TILES[ti]
                    if ti in qtiles:
                        b = cnt % 2; cnt += 1
                        T.mm([(ps(b)[:96, :tn], wq[:, kc, hh * 96:(hh + 1) * 96], qaT[:, kc, t0:t0 + tn], kc == 0, kc == 2) for kc in range(3)],
                             reads=[('slot', sq_), ('qaT', ti)], writes=[pk(b)])
                        T.op('act', lambda e: e.activation(out=xq[b][:96, :tn], in_=ps(b)[:96, :tn], func=AF.Copy), reads=[pk(b)], writes=[('xq', b)])
                        rope(xq[b], 96, tn, t0, QhT[0:96, hh, t0:t0 + tn], cosT, sinT, Rm, (t1[b], t2[b]), b, [('xq', b)], [('QhT', hh, ti)])
                    b = cnt % 2; cnt += 1
                    T.mm([(ps(b)[:64, :tn], wkv[:, kc, hh * 128:hh * 128 + 64], kvT[:, kc, t0:t0 + tn], kc == 0, kc == 1) for kc in range(2)],
                         reads=[('slot', sk_), ('kvaT', ti)], writes=[pk(b)])
                    T.op('act', lambda e: e.activation(out=KhT[0:64, hh, t0:t0 + tn], in_=ps(b)[:64, :tn], func=AF.Copy),
                         reads=[pk(b)], writes=[('KhT', hh, ti, 0)])
                    T.op('pool', lambda e: e.tensor_copy(out=KhT[64:96, hh, t0:t0 + tn], in_=krT[64:96, t0:t0 + tn]),
                         reads=[('krT', ti)], writes=[('KhT', hh, ti, 1)])
            wv3 = wkv.rearrange("p k (h x) -> p k h x", h=2, x=128)
            for c in range(18):
                b = cnt % 2; cnt += 1
                T.mm([(ps(b)[:, 0:128], kvT[:, kc, c * 128:(c + 1) * 128], wv3[:, kc, :, 64:128], kc == 0, kc == 1) for kc in range(2)],
                     reads=[('slot', sk_), ('kvaT', c // 4)], writes=[pk(b)])
                T.op('dve', lambda e, c=c: e.tensor_copy(out=va[:, c, 0, 0:64], in_=ps(b)[:, 0:64]), reads=[pk(b)], writes=[('va', c, 0)])
                T.op('act', lambda e, c=c: e.activation(out=va[:, c, 1, 64:128], in_=ps(b)[:, 64:128], func=AF.Copy), reads=[pk(b)], writes=[('va', c, 1)])
            for hh in range(2):
                jobs = [(ti, list(range(18))) for ti in range(4)]
                if with_ctx:
                    jobs.append((4, [16, 17]))
                for ti, chunks in jobs:
                    t0, tn = TILES[ti]
                    attention(lambda c: KhT[0:96, hh, c * 128:(c + 1) * 128], QhT[0:96, hh, t0:t0 + tn],
                              lambda c: va[:, c, hh, :], chunks, tn, 96.0 ** -0.5, hh, oT[64 * hh:64 * hh + 64, t0:t0 + tn],
                              [('QhT', hh, ti), ('KhT', hh), 'va'], [('oT', ti, hh)], (pT, dt, acnt))
            T.dma('sp', o_scr[3][:, pp, :], oT, reads=['oT'], writes=[('o_scr', 3, pp)])

    def na(li, with_ctx, stage=9):
        arena_big()
        qT = alloc([4, TT], BF16); kT = alloc([4, TT], BF16); vE = alloc([18, 512], BF16); vO = alloc([15, 512], BF16)
        bt = [alloc([2, 2, 7, 64], F32) for _ in range(2)]
        oT = alloc([4, TT], BF16)
        sbf = [alloc([2, 4, 64], F32) for _ in range(2)]
        pTn = [alloc([2, 6, 64], BF16) for _ in range(2)]
        rr = [alloc([128], F32) for _ in range(2)]
        pTc = alloc([2, 2, 256], BF16); rc = alloc([512], F32)
        if not with_ctx:
            T.op('pool', lambda e: e.memset(oT[:, :, S:TT], 0.0), writes=[('oT', 'ctxz')])
        cnt = 0
        for (col0, dstT, nm) in ((O_NQ, qT, 'qT'), (O_NK, kT, 'kT')):
            si = slot(); w = sview(si, [8, 512]); wload(si, w, kc_rows(w_in[li], col0, 512))
            for c in range(4):
                for ti in range(5):
                    t0, tn = TILES[ti]
                    b = cnt % 2; cnt += 1
                    T.mm([(ps(b)[:, :tn], w[:, kc, c * 128:(c + 1) * 128], nT[:, kc, t0:t0 + tn], kc == 0, kc == 7) for kc in range(8)],
                         reads=[('slot', si), ('nT', ti)], writes=[pk(b)])
                    if b == 0:
                        T.op('dve', lambda e: e.tensor_copy(out=dstT[:, c, t0:t0 + tn], in_=ps(b)[:, :tn]), reads=[pk(b)], writes=[(nm, c, ti)])
                    else:
                        T.op('act', lambda e: e.activation(out=dstT[:, c, t0:t0 + tn], in_=ps(b)[:, :tn], func=AF.Copy), reads=[pk(b)], writes=[(nm, c, ti)])
        si = slot(); wv = sview(si, [8, 512]); wload(si, wv, kc_rows(w_in[li], O_NV, 512))
        vjobs = [(vE[:, c, :], ('vE', c), 128 * c) for c in range(16)] + [(vE[:, 16 + c, :], ('vE', 16 + c), S + 128 * c) for c in range(2)] \
            + [(vO[:, c, :], ('vO', c), 64 + 128 * c) for c in range(15)]
        for dst, key, tok in vjobs:
            b = cnt % 2; cnt += 1
            tis = sorted(set([min(tok // 512, 4), min((tok + 127) // 512, 4)]))
            T.mm([(ps(b), nT[:, kc, tok:tok + 128], wv[:, kc, :], kc == 0, kc == 7) for kc in range(8)],
                 reads=[('slot', si)] + [('nT', t_) for t_ in tis], writes=[pk(b)])
            if b == 0:
                T.op('dve', lambda e: e.tensor_copy(out=dst, in_=ps(b)), reads=[pk(b)], writes=[key])
            else:
                T.op('act', lambda e: e.activation(out=dst, in_=ps(b), func=AF.Copy), reads=[pk(b)], writes=[key])
        if stage < 2:
            T.dma('sp', o_scr[2], qT, reads=['qT'], writes=[('o_scr', 2)])
            return
        it = 0
        for pp in range(4 if stage >= 3 else 1):
            btp = bt[pp % 2]
            T.dma('sp', btp, na_tab[li, 2 * pp:2 * pp + 2].rearrange("h p r j q -> p h r j q"), writes=[('bt', pp % 2)])
            for qr in range(32):
                start = min(max(qr - 4, 0), 24)
                a0 = start - qr + 7
                q0 = qr * 64
                ktoks = [(start + 2 * i) * 64 for i in range(4)] + [S, S + 128]
                if start % 2 == 0:
                    vts = [(vE[:, (start + 2 * i) // 2, pp * 128:(pp + 1) * 128], ('vE', (start + 2 * i) // 2)) for i in range(4)]
                else:
                    vts = [(vO[:, (start + 2 * i - 1) // 2, pp * 128:(pp + 1) * 128], ('vO', (start + 2 * i - 1) // 2)) for i in range(4)]
                vts += [(vE[:, 16, pp * 128:(pp + 1) * 128], ('vE', 16)), (vE[:, 17, pp * 128:(pp + 1) * 128], ('vE', 17))]
                x = it % 2
                it += 1
                Sx = PP[x]
                mms = []
                for hh in range(2):
                    off = 64 * hh
                    for i in range(6):
                        mms.append((Sx[:, hh * 512 + i * 64:hh * 512 + i * 64 + 64], kT[off:off + 64, pp, ktoks[i]:ktoks[i] + 128],
                                    qT[off:off + 64, pp, q0:q0 + 64], True, True))
                T.mm(mms, reads=[('kT', pp), ('qT', pp, qr // 8)], writes=[pk(2 * x), pk(2 * x + 1)])
                S3 = Sx.rearrange("p (h c) -> p h c", h=2, c=512)
                sb_ = sbf[x]; pt = pTn[x]
                T.op('dve', lambda e: e.scalar_tensor_tensor(
                    out=sb_, in0=S3[:, :, 0:256].rearrange("p h (i q) -> p h i q", i=4, q=64), scalar=0.125,
                    in1=btp[:, :, a0 % 2, a0 // 2:a0 // 2 + 4, :], op0=ALU.mult, op1=ALU.add),
                    reads=[pk(2 * x), pk(2 * x + 1), ('bt', pp % 2)], writes=[('sbf', x)])
                T.op('act', lambda e: e.activation(out=pt[:, :, 0:4, :], in_=sb_, func=AF.Exp), reads=[('sbf', x)], writes=[('pTn', x, 0)])
                T.op('act', lambda e: e.activation(out=pt[:, :, 4:6, :], in_=S3[:, :, 256:384].rearrange("p h (i q) -> p h i q", i=2, q=64),
                                                   func=AF.Exp, scale=0.125), reads=[pk(2 * x), pk(2 * x + 1)], writes=[('pTn', x, 1)])
                po = ps(4 + 2 * x)[:, 0:128]; pd = ps(5 + 2 * x)[:, 0:128]
                mms = []
                for i in range(6):
                    mms.append((po, vts[i][0], pt[:, :, i, :], i == 0, i == 5))
                    mms.append((pd, onesb, pt[:, :, i, :], i == 0, i == 5))
                T.mm(mms, reads=[('pTn', x), 'onesb'] + [v[1] for v in vts], writes=[pk(4 + 2 * x), pk(5 + 2 * x)])
                r_ = rr[x]
                T.op('dve', lambda e: e.reciprocal(out=r_, in_=pd), reads=[pk(5 + 2 * x)], writes=[('rr', x)])
                for hh in range(2):
                    sl = slice(64 * hh, 64 * hh + 64)
                    T.op('dve', lambda e: e.tensor_tensor(out=oT[sl, pp, q0:q0 + 64], in0=po[sl, 64 * hh:64 * hh + 64], in1=r_[sl, 64 * hh:64 * hh + 64], op=ALU.mult),
                         reads=[pk(4 + 2 * x), ('rr', x)], writes=[('oT', pp, qr, hh)])
            if with_ctx:
                x = it % 2
                it += 1
                Sx = PP[x]
                mms = []
                for hh in range(2):
                    off = 64 * hh
                    for i in range(2):
                        mms.append((Sx[:, hh * 512 + i * 256:hh * 512 + i * 256 + 256], kT[off:off + 64, pp, S + 128 * i:S + 128 * i + 128],
                                    qT[off:off + 64, pp, S:TT], True, True))
                T.mm(mms, reads=[('kT', pp), ('qT', pp, 4)], writes=[pk(2 * x), pk(2 * x + 1)])
                T.op('act', lambda e: e.activation(out=pTc, in_=Sx.rearrange("p (h i q) -> p h i q", h=2, i=2, q=256), func=AF.Exp, scale=0.125),
                     reads=[pk(2 * x), pk(2 * x + 1)], writes=['pTc'])
                po = ps(4 + 2 * x); pd = ps(5 + 2 * x)
                mms = []
                for i in range(2):
                    mms.append((po, vE[:, 16 + i, pp * 128:(pp + 1) * 128], pTc[:, :, i, :], i == 0, i == 1))
                    mms.append((pd, onesb, pTc[:, :, i, :], i == 0, i == 1))
                T.mm(mms, reads=['pTc', 'onesb', ('vE', 16), ('vE', 17)], writes=[pk(4 + 2 * x), pk(5 + 2 * x)])
                T.op('dve', lambda e: e.reciprocal(out=rc, in_=pd), reads=[pk(5 + 2 * x)], writes=['rc'])
                for hh in range(2):
                    sl = slice(64 * hh, 64 * hh + 64)
                    T.op('dve', lambda e: e.tensor_tensor(out=oT[sl, pp, S:TT], in0=po[sl, 256 * hh:256 * hh + 256], in1=rc[sl, 256 * hh:256 * hh + 256], op=ALU.mult),
                         reads=[pk(4 + 2 * x), 'rc'], writes=[('oT', pp, 'c', hh)])
        T.dma('sp', o_scr[2], oT, reads=['oT'], writes=[('o_scr', 2)])

    def hyena(li, with_ctx, stage=9):
        arena_big()
        for sb_, d_, nm in ((hycw, hy_cw[li], 'hycw'), (hycb, hy_cb[li], 'hycb'), (hysk, hy_skip[li], 'hysk')):
            T.dma('sp', sb_, d_, writes=[nm])
        T.dma('sp', hyfb[:64], hy_fb[li], writes=['hyfb']); T.dma('sp', hyff[:64], hy_ff[li], writes=['hyff'])
        T.op('dve', lambda e: e.tensor_tensor(out=hyfbf[:64], in0=hyfb[:64], in1=hyff[:64], op=ALU.mult), reads=['hyfb', 'hyff'], writes=['hyfbf'])
        vxT = alloc([18, 512], BF16)
        R1 = state['off']
        tiles = range(5) if with_ctx else range(4)
        segs = [(0, S)] + ([(S, CT)] if with_ctx else [])
        ntok = TT if with_ctx else S
        uraw = [alloc([TT], F32) for _ in range(3)]
        ucv = [alloc([TT], F32) for _ in range(3)]
        vxb = alloc([TT], BF16)
        ws = []
        for r in range(3):
            si = slot(); w = sview(si, [8, 512]); wload(si, w, kc_rows(w_in[li], O_HY + 512 * r, 512)); ws.append((si, w))
        cnt = 0
        for cc in range(4):
            for r in range(3):
                ch = r * 4 + cc
                si, w = ws[r]
                for ti in tiles:
                    t0, tn = TILES[ti]
                    b = cnt % 2; cnt += 1
                    T.mm([(ps(b)[:, :tn], w[:, kc, cc * 128:(cc + 1) * 128], nT[:, kc, t0:t0 + tn], kc == 0, kc == 7) for kc in range(8)],
                         reads=[('slot', si), ('nT', ti)], writes=[pk(b)])
                    T.op('act', lambda e: e.activation(out=uraw[r][:, t0:t0 + tn], in_=ps(b)[:, :tn], func=AF.Copy), reads=[pk(b)], writes=[('uraw', r, ti)])
                T.op('act', lambda e: e.activation(out=ucv[r][:, :ntok], in_=uraw[r][:, :ntok], func=AF.Identity,
                                                   bias=hycb[:, ch:ch + 1], scale=hycw[:, ch, 1:2]),
                     reads=[('uraw', r), 'hycw', 'hycb'], writes=[('ucv', r)])
                for (s0, L) in segs:
                    T.op('dve', lambda e: e.scalar_tensor_tensor(out=ucv[r][:, s0 + 1:s0 + L], in0=uraw[r][:, s0:s0 + L - 1], scalar=hycw[:, ch, 0:1],
                                                                  in1=ucv[r][:, s0 + 1:s0 + L], op0=ALU.mult, op1=ALU.add),
                         reads=[('uraw', r), 'hycw', ('ucv', r)], writes=[('ucv', r)])
                    T.op('dve', lambda e: e.scalar_tensor_tensor(out=ucv[r][:, s0:s0 + L - 1], in0=uraw[r][:, s0 + 1:s0 + L], scalar=hycw[:, ch, 2:3],
                                                                  in1=ucv[r][:, s0:s0 + L - 1], op0=ALU.mult, op1=ALU.add),
                         reads=[('uraw', r), 'hycw', ('ucv', r)], writes=[('ucv', r)])
            T.dma('sp', hy_x0[cc][:, :ntok], ucv[0][:, :ntok], reads=[('ucv', 0)], writes=[('hy_x0', cc)])
            T.op('pool', lambda e: e.tensor_tensor(out=ucv[2][:, :ntok], in0=ucv[2][:, :ntok], in1=ucv[1][:, :ntok], op=ALU.mult),
                 reads=[('ucv', 1), ('ucv', 2)], writes=[('ucv', 2)])
            T.dma('sp', hy_vx[cc][:, :ntok], ucv[2][:, :ntok], reads=[('ucv', 2)], writes=[('hy_vx', cc)])
            T.op('act', lambda e: e.activation(out=vxb[:, :ntok], in_=ucv[2][:, :ntok], func=AF.Copy), reads=[('ucv', 2)], writes=['vxb'])
            for g4 in range(5 if with_ctx else 4):
                n4 = 4 if g4 < 4 else 2
                b = cnt % 2; cnt += 1
                pb = ps(b).bitcast(BF16)
                for j in range(n4):
                    tt = g4 * 4 + j
                    T.tr(pb[:, j * 128:(j + 1) * 128], vxb[:, tt * 128:(tt + 1) * 128], identb, reads=['vxb', 'identb'], writes=[pk(b)])
                T.op('dve', lambda e: e.tensor_copy(out=vxT[:, g4 * 4:g4 * 4 + n4, cc * 128:(cc + 1) * 128],
                                                    in_=pb[:, 0:n4 * 128].rearrange("p (a b) -> p a b", a=n4, b=128)),
                     reads=[pk(b)], writes=[('vxT', g4, cc)])
        T.barrier(keep=KEEP + ('vxT', 'hy_x0', 'hy_vx', 'hycw', 'hycb', 'hysk', 'hyff', 'hyfbf'))
        if stage < 2:
            T.dma('sp', o_scr[1][:, :, 0:2048], vxT[:, 0:16, :].rearrange("p a b -> p (a b)").rearrange("p (a b) -> p a b", a=4, b=2048), reads=['vxT'], writes=[('o_scr', 1)])
            return
        oT_off = R1 + 4 * 16384
        for (s0, L) in segs:
            nt = L // 128
            tc0 = s0 // 128
            LS = str(L)
            inv = hinv
            state['off'] = R1
            ksT = alloc([nt, 512], BF16); kdT = alloc([nt, 512], BF16)
            state['off'] = R1 + 2 * 16384
            absk = alloc([nt, 512], BF16)
            zT = alloc([L], BF16); fw1 = alloc([64], BF16); wmid = alloc([2, 64], BF16); wout = alloc([1024], BF16)
            hd = [alloc([L], BF16) for _ in range(2)]
            arg = [alloc([512], F32) for _ in range(2)]; kq = [alloc([512], F32) for _ in range(2)]
            ki = [alloc([512], I32) for _ in range(2)]
            dF = [alloc([512], F32) for _ in range(2)]; dB = [alloc([512], F32) for _ in range(2)]
            tA = [alloc([512], F32) for _ in range(2)]; tB = [alloc([512], F32) for _ in range(2)]
            knyq = alloc([512], F32)
            T.dma('sp', zT[:33], cst['zT' + LS], writes=['zT'])
            stg1 = alloc([64], F32); stg2 = alloc([128], F32); stg3 = alloc([1024], F32)
            T.dma('sp', stg1[:33], hy_fw1[li], writes=['stg1'])
            T.dma('sp', stg2[:64], hy_wmid[li].rearrange("k j m -> k (j m)"), writes=['stg2'])
            T.dma('sp', stg3[:64], hy_wout[li], writes=['stg3'])
            T.op('act', lambda e: e.activation(out=fw1[:33], in_=stg1[:33], func=AF.Copy), reads=['stg1'], writes=['fw1'])
            T.op('act', lambda e: e.activation(out=wmid[:64].rearrange("p a b -> p (a b)"), in_=stg2[:64], func=AF.Copy), reads=['stg2'], writes=['wmid'])
            T.op('act', lambda e: e.activation(out=wout[:64], in_=stg3[:64], func=AF.Copy), reads=['stg3'], writes=['wout'])
            tn = min(512, L)
            cnt = 0
            for j in range(3):
                for tl in range(L // tn):
                    c0 = tl * tn
                    b = cnt % 2; cnt += 1
                    if j == 0:
                        T.mm([(ps(b)[:64, :tn], fw1[:33, :], zT[:33, c0:c0 + tn], True, True)], reads=['fw1', 'zT'], writes=[pk(b)])
                    else:
                        T.mm([(ps(b)[:64, :tn], wmid[:64, j - 1, :], hd[(j - 1) % 2][:64, c0:c0 + tn], True, True)],
                             reads=['wmid', ('hd', (j - 1) % 2)], writes=[pk(b)])
                    a_ = arg[b]; q_ = kq[b]; i_ = ki[b]
                    T.op('act', lambda e: e.activation(out=a_[:64, :tn], in_=ps(b)[:64, :tn], func=AF.Identity,
                                                       bias=hyfbf[:64, j:j + 1], scale=hyff[:64, j:j + 1]),
                         reads=[pk(b), 'hyfbf', 'hyff'], writes=[('arg', b)])
                    T.op('dve', lambda e: e.tensor_scalar_mul(out=q_[:64, :tn], in0=a_[:64, :tn], scalar1=1.0 / TWO_PI), reads=[('arg', b)], writes=[('kq', b)])
                    T.op('dve', lambda e: e.tensor_copy(out=i_[:64, :tn], in_=q_[:64, :tn]), reads=[('kq', b)], writes=[('ki', b)])
                    T.op('dve', lambda e: e.tensor_copy(out=q_[:64, :tn], in_=i_[:64, :tn]), reads=[('ki', b)], writes=[('kq', b)])
                    T.op('dve', lambda e: e.scalar_tensor_tensor(out=a_[:64, :tn], in0=q_[:64, :tn], scalar=-TWO_PI, in1=a_[:64, :tn],
                                                                  op0=ALU.mult, op1=ALU.add), reads=[('kq', b), ('arg', b)], writes=[('arg', b)])
                    T.op('dve', lambda e: e.tensor_scalar(out=a_[:64, :tn], in0=a_[:64, :tn], scalar1=-3.141592, scalar2=3.141592,
                                                           op0=ALU.max, op1=ALU.min), reads=[('arg', b)], writes=[('arg', b)])
                    T.op('act', lambda e: e.activation(out=hd[j % 2][:64, c0:c0 + tn], in_=a_[:64, :tn], func=AF.Sin),
                         reads=[('arg', b)], writes=[('hd', j % 2, tl)])
            h3 = hd[0]
            for tc in range(nt):
                b = cnt % 2; cnt += 1
                x = tc % 2
                T.dma('sp', dF[x], cst['decF' + LS][:, tc, :], writes=[('dF', x)])
                T.dma('sp', dB[x], cst['decB' + LS][:, tc, :], writes=[('dB', x)])
                T.mm([(ps(2 * b), h3[:64, tc * 128:(tc + 1) * 128], wout[:64, 0:512], True, True),
                      (ps(2 * b + 1), h3[:64, tc * 128:(tc + 1) * 128], wout[:64, 512:1024], True, True)],
                     reads=[('hd', 0), 'wout'], writes=[pk(2 * b), pk(2 * b + 1)])
                T.op('dve', lambda e: e.tensor_tensor(out=tA[x], in0=ps(2 * b), in1=dF[x], op=ALU.mult), reads=[pk(2 * b), ('dF', x)], writes=[('tA', x)])
                T.op('dve', lambda e: e.tensor_tensor(out=tB[x], in0=ps(2 * b + 1), in1=dB[x], op=ALU.mult), reads=[pk(2 * b + 1), ('dB', x)], writes=[('tB', x)])
                T.op('pool', lambda e: e.tensor_tensor(out=ksT[:, tc, :], in0=tA[x], in1=tB[x], op=ALU.add), reads=[('tA', x), ('tB', x)], writes=[('ksT', tc)])
                T.op('pool', lambda e: e.tensor_tensor(out=kdT[:, tc, :], in0=tA[x], in1=tB[x], op=ALU.subtract), reads=[('tA', x), ('tB', x)], writes=[('kdT', tc)])
                T.op('act', lambda e: e.activation(out=tA[x], in_=tA[x], func=AF.Abs), reads=[('tA', x)], writes=[('tA', x)])
                T.op('act', lambda e: e.activation(out=tB[x], in_=tB[x], func=AF.Abs), reads=[('tB', x)], writes=[('tB', x)])
                T.op('pool', lambda e: e.tensor_tensor(out=absk[:, tc, :], in0=tA[x], in1=tB[x], op=ALU.add), reads=[('tA', x), ('tB', x)], writes=[('absk', tc)])
            for cc in range(4):
                T.mm([(ps(4)[:, cc:cc + 1], absk[:, tc, cc * 128:(cc + 1) * 128], onesb[:, 0:1], tc == 0, tc == nt - 1) for tc in range(nt)],
                     reads=['absk', 'onesb'], writes=[pk(4)])
            T.op('dve', lambda e: e.reciprocal(out=inv, in_=ps(4)[:, 0:4]), reads=[pk(4)], writes=['hinv'])
            sS0 = slot(); SF0 = sview(sS0, [nt, 128])
            T.dma('sp', SF0, cst['SF' + LS][0], writes=[('slot', sS0)])
            T.mm([(ps(5)[0:1, :], SF0[:, tc, 0:1], ksT[:, tc, :], tc == 0, tc == nt - 1) for tc in range(nt)],
                 reads=[('slot', sS0), 'ksT'], writes=[pk(5)])
            T.op('dve', lambda e: e.tensor_copy(out=knyq[0:1, :], in_=ps(5)[0:1, :]), reads=[pk(5)], writes=['knyq'])
            T.barrier(keep=KEEP + ('vxT', 'hy_x0', 'hy_vx', 'hycw', 'hycb', 'hysk', 'hinv', 'ksT', 'kdT', 'knyq', 'oT'))
            if stage < 3:
                T.dma('sp', o_scr[1][:, 0:2, 0:nt * 256].rearrange("p a b -> p (a b)"), ksT.rearrange("p a b -> p (a b)"), reads=['ksT'], writes=[('o_scr', 1)])
                T.dma('sp', o_scr[1][:, 2:4, 0:nt * 256].rearrange("p a b -> p (a b)"), kdT.rearrange("p a b -> p (a b)"), reads=['kdT'], writes=[('o_scr', 1, 1)])
                return
            state['off'] = R1 + 2 * 16384
            Yre = alloc([nt, 512], BF16); Yim = alloc([nt, 512], BF16)
            state['off'] = oT_off
            krs = [alloc([512], F32) for _ in range(2)]; kis = [alloc([512], F32) for _ in range(2)]
            pa = [alloc([512], F32) for _ in range(2)]; pb_ = [alloc([512], F32) for _ in range(2)]
            pc = [alloc([512], F32) for _ in range(2)]; pd_ = [alloc([512], F32) for _ in range(2)]
            for fc in range(nt):
                sC = slot(); CFb = sview(sC, [nt, 128]); T.dma('sp', CFb, cst['CF' + LS][fc], writes=[('slot', sC)])
                sS = slot(); SFb = sview(sS, [nt, 128]); T.dma('sp', SFb, cst['SF' + LS][fc], writes=[('slot', sS)])
                x = fc % 2
                B0 = 4 * x
                mms = []
                for tc in range(nt):
                    st_, sp_ = tc == 0, tc == nt - 1
                    mms.append((ps(B0), CFb[:, tc, :], vxT[:, tc0 + tc, :], st_, sp_))
                    mms.append((ps(B0 + 1), CFb[:, tc, :], ksT[:, tc, :], st_, sp_))
                    mms.append((ps(B0 + 2), SFb[:, tc, :], vxT[:, tc0 + tc, :], st_, sp_))
                    mms.append((ps(B0 + 3), SFb[:, tc, :], kdT[:, tc, :], st_, sp_))
                T.mm(mms, reads=[('slot', sC), ('slot', sS), 'vxT', 'ksT', 'kdT'], writes=[pk(B0), pk(B0 + 1), pk(B0 + 2), pk(B0 + 3)])
                T.op('act', lambda e: e.activation(out=krs[x], in_=ps(B0 + 1), func=AF.Copy), reads=[pk(B0 + 1)], writes=[('krs', x)])
                T.op('act', lambda e: e.activation(out=kis[x], in_=ps(B0 + 3), func=AF.Copy), reads=[pk(B0 + 3)], writes=[('kis', x)])
                T.op('dve', lambda e: e.tensor_tensor(out=pa[x], in0=ps(B0), in1=krs[x], op=ALU.mult), reads=[pk(B0), ('krs', x)], writes=[('pa', x)])
                T.op('dve', lambda e: e.tensor_tensor(out=pb_[x], in0=ps(B0 + 2), in1=kis[x], op=ALU.mult), reads=[pk(B0 + 2), ('kis', x)], writes=[('pb', x)])
                T.op('dve', lambda e: e.tensor_tensor(out=pc[x], in0=ps(B0), in1=kis[x], op=ALU.mult), reads=[pk(B0), ('kis', x)], writes=[('pc', x)])
                T.op('dve', lambda e: e.tensor_tensor(out=pd_[x], in0=ps(B0 + 2), in1=krs[x], op=ALU.mult), reads=[pk(B0 + 2), ('krs', x)], writes=[('pd', x)])
                T.op('pool', lambda e: e.tensor_tensor(out=Yre[:, fc, :], in0=pa[x], in1=pb_[x], op=ALU.subtract), reads=[('pa', x), ('pb', x)], writes=[('Yre', fc)])
                T.op('pool', lambda e: e.tensor_tensor(out=Yim[:, fc, :], in0=pc[x], in1=pd_[x], op=ALU.add), reads=[('pc', x), ('pd', x)], writes=[('Yim', fc)])
                if fc == 0:
                    T.op('pool', lambda e: e.tensor_copy(out=Yre[0:1, 0, :], in_=pa[x][0:1, :]), reads=[('pa', x), ('Yre', 0)], writes=[('Yre', 0)])
                    T.op('dve', lambda e: e.tensor_tensor(out=Yim[0:1, 0, :], in0=ps(B0 + 2)[0:1, :], in1=knyq[0:1, :], op=ALU.mult),
                         reads=[pk(B0 + 2), 'knyq', ('Yim', 0)], writes=[('Yim', 0)])
            T.barrier(keep=KEEP + ('vxT', 'hy_x0', 'hy_vx', 'hycw', 'hycb', 'hysk', 'hinv', 'Yre', 'Yim', 'oT'))
            state['off'] = oT_off
            ot = [alloc([256], BF16) for _ in range(2)]
            x0t = [alloc([256], F32) for _ in range(2)]; vxt = [alloc([256], F32) for _ in range(2)]
            e1 = [alloc([256], F32) for _ in range(2)]; e2 = [alloc([256], F32) for _ in range(2)]
            if not with_ctx and s0 == 0:
                zt = alloc([256], BF16)
                T.op('pool', lambda e: e.memset(zt, 0.0), writes=['zt'])
                for cc in range(4):
                    T.dma('sp', o_scr[1][:, cc, S:TT], zt, reads=['zt'], writes=[('o_scr', 1, cc, 'z')])
            cnt = 0
            for th in range(L // 256):
                c0 = th * 256
                sI = slot(); ICs = sview(sI, [nt, 256]); T.dma('sp', ICs, cst['IC' + LS][:, :, c0:c0 + 256], writes=[('slot', sI)])
                sJ = slot(); ISs = sview(sJ, [nt, 256]); T.dma('sp', ISs, cst['IS' + LS][:, :, c0:c0 + 256], writes=[('slot', sJ)])
                for cc in range(4):
                    b = cnt % 2; cnt += 1
                    T.dma('sp', x0t[b], hy_x0[cc][:, s0 + c0:s0 + c0 + 256], reads=[('hy_x0', cc)], writes=[('x0t', b)])
                    T.dma('sp', vxt[b], hy_vx[cc][:, s0 + c0:s0 + c0 + 256], reads=[('hy_vx', cc)], writes=[('vxt', b)])
                    mms = []
                    for fc in range(nt):
                        mms.append((ps(b)[:, 0:256], Yre[:, fc, cc * 128:(cc + 1) * 128], ICs[:, fc, :], fc == 0, False))
                        mms.append((ps(b)[:, 0:256], Yim[:, fc, cc * 128:(cc + 1) * 128], ISs[:, fc, :], False, fc == nt - 1))
                    T.mm(mms, reads=['Yre', 'Yim', ('slot', sI), ('slot', sJ)], writes=[pk(b)])
                    T.op('pool', lambda e: e.tensor_scalar(out=e1[b], in0=vxt[b], scalar1=hysk[:, cc:cc + 1], scalar2=None, op0=ALU.mult),
                         reads=[('vxt', b), 'hysk'], writes=[('e1', b)])
                    T.op('dve', lambda e: e.scalar_tensor_tensor(out=e2[b], in0=ps(b)[:, 0:256], scalar=inv[:, cc:cc + 1], in1=e1[b],
                                                                  op0=ALU.mult, op1=ALU.add), reads=[pk(b), 'hinv', ('e1', b)], writes=[('e2', b)])
                    T.op('pool', lambda e: e.tensor_tensor(out=ot[b], in0=e2[b], in1=x0t[b], op=ALU.mult),
                         reads=[('e2', b), ('x0t', b)], writes=[('ot', b)])
                    T.dma('sp', o_scr[1][:, cc, s0 + c0:s0 + c0 + 256], ot[b], reads=[('ot', b)], writes=[('o_scr', 1, cc, s0 + c0)])
            T.barrier(keep=KEEP + ('vxT', 'hy_x0', 'hy_vx', 'hycw', 'hycb', 'hysk'))

    def merge(li, with_ctx):
        arena_big()
        mT = alloc([8, TT], F32)
        oT = alloc([4, TT], BF16)
        sg = [alloc([512], F32) for _ in range(2)]; tm = [alloc([512], F32) for _ in range(2)]
        tiles = range(5) if with_ctx else range(4)

        def load(br):
            s3 = (slot(), slot(), slot())
            g0 = sview(s3[0], [8, 512]); g1 = sview(s3[1], [8, 512]); wb = sview(s3[2], [4, D])
            wload(s3[0], g0, kc_rows(w_in[li], O_GATE + br * 1024, 512))
            wload(s3[1], g1, kc_rows(w_in[li], O_GATE + br * 1024 + 512, 512))
            wload(s3[2], wb, w_branch[li, br].rearrange("(kc p) n -> p kc n", p=128))
            return s3, (g0, g1, wb)
        nxt = load(0)
        cnt = 0
        for br in range(4):
            s3, (g0, g1, wb) = nxt
            if br + 1 < 4:
                nxt = load(br + 1)
            T.dma('sp', oT, o_scr[br], reads=[('o_scr', br)], writes=['oT'])
            for m in range(8):
                wg = g0 if m < 4 else g1
                sgi = s3[0] if m < 4 else s3[1]
                for ti in tiles:
                    t0, tn = TILES[ti]
                    b = cnt % 2; cnt += 1
                    T.mm([(ps(b)[:, :tn], wg[:, kc, (m % 4) * 128:(m % 4 + 1) * 128], nT[:, kc, t0:t0 + tn], kc == 0, kc == 7) for kc in range(8)],
                         reads=[('slot', sgi), ('nT', ti)], writes=[pk(b)])
                    T.mm([(ps(2 + b)[:, :tn], wb[:, kc, m * 128:(m + 1) * 128], oT[:, kc, t0:t0 + tn], kc == 0, kc == 3) for kc in range(4)],
                         reads=[('slot', s3[2]), 'oT'], writes=[pk(2 + b)])
                    T.op('act', lambda e: e.activation(out=sg[b][:, :tn], in_=ps(b)[:, :tn], func=AF.Sigmoid), reads=[pk(b)], writes=[('sg', b)])
                    if br == 0:
                        T.op('dve', lambda e: e.tensor_tensor(out=mT[:, m, t0:t0 + tn], in0=sg[b][:, :tn], in1=ps(2 + b)[:, :tn], op=ALU.mult),
                             reads=[('sg', b), pk(2 + b)], writes=[('mT', ti, m)])
                    else:
                        T.op('dve', lambda e: e.tensor_tensor(out=tm[b][:, :tn], in0=sg[b][:, :tn], in1=ps(2 + b)[:, :tn], op=ALU.mult),
                             reads=[('sg', b), pk(2 + b)], writes=[('tm', b)])
                        T.op('pool', lambda e: e.tensor_tensor(out=mT[:, m, t0:t0 + tn], in0=mT[:, m, t0:t0 + tn], in1=tm[b][:, :tn], op=ALU.add),
                             reads=[('tm', b), ('mT', ti, m)], writes=[('mT', ti, m)])
        T.barrier(keep=KEEP + ('mT', 'h_spill'))
        state['off'] = small_off
        hl = alloc([8, 512], F32); mb = alloc([8, 512], BF16)
        so = (slot(), slot())
        wo = [sview(so[0], [8, 512]), sview(so[1], [8, 512])]
        wload(so[0], wo[0], kc_rows(w_out[li], 0, 512)); wload(so[1], wo[1], kc_rows(w_out[li], 512, 512))
        for ti in tiles:
            t0, tn = TILES[ti]
            lc = 0 if ti < 4 else 1
            T.dma('sp', hl[:, :, :tn], h_spill[:, :, t0:t0 + tn], reads=['h_spill'], writes=['hl'])
            T.op('act', lambda e: e.activation(out=mb[:, :, :tn], in_=mT[:, :, t0:t0 + tn], func=AF.Copy), reads=[('mT', ti)], writes=['mb'])
            for m2 in range(8):
                b = 4 + m2 % 2
                T.mm([(ps(b)[:, :tn], wo[m2 // 4][:, kc, (m2 % 4) * 128:(m2 % 4 + 1) * 128], mb[:, kc, :tn], kc == 0, kc == 7) for kc in range(8)],
                     reads=[('slot', so[m2 // 4]), 'mb'], writes=[pk(b)])
                T.op('dve', lambda e: e.scalar_tensor_tensor(out=hl[:, m2, :tn], in0=ps(b)[:, :tn], scalar=mcol(lc, 5, m2), in1=hl[:, m2, :tn],
                                                              op0=ALU.mult, op1=ALU.add), reads=[pk(b), 'mod', ('hl', m2)], writes=[('hl', m2)])
            T.dma('sp', h_spill[:, :, t0:t0 + tn], hl[:, :, :tn], reads=['hl'], writes=[('h_spill', ti)])
        T.barrier(keep=KEEP)
        T.dma('sp', hT, h_spill, reads=['h_spill'], writes=['hT'])

    def final():
        arena_small()
        fnT = alloc([8], F32)
        T.dma('sp', fnT, fnorm_d, writes=['fnT'])
        sq = alloc([8, 512], BF16)
        rs = alloc([512], F32)
        yT = alloc([8, 512], F32)
        ot = [alloc([D], F32) for _ in range(2)]
        cnt = 0
        for ti in range(4):
            t0, tn = TILES[ti]
            T.op('act', lambda e: e.activation(out=sq[:, :, :tn], in_=hT[:, :, t0:t0 + tn], func=AF.Square), reads=[('hT', ti)], writes=['sq'])
            T.mm([(ps(6)[:, :tn], onesb, sq[:, kc, :tn], kc == 0, kc == 7) for kc in range(8)], reads=['sq', 'onesb'], writes=[pk(6)])
            T.op('act', lambda e: e.activation(out=rs[:, :tn], in_=ps(6)[:, :tn], func=AF.Sqrt, bias=EPS, scale=1.0 / D), reads=[pk(6)], writes=['rs'])
            T.op('dve', lambda e: e.reciprocal(out=rs[:, :tn], in_=rs[:, :tn]), reads=['rs'], writes=['rs'])
            for kc in range(8):
                T.op('dve', lambda e, kc=kc: e.scalar_tensor_tensor(out=yT[:, kc, :tn], in0=hT[:, kc, t0:t0 + tn], scalar=fnT[:, kc:kc + 1], in1=rs[:, :tn],
                                                                    op0=ALU.mult, op1=ALU.mult), reads=[('hT', ti), 'rs', 'fnT'], writes=[('yT', kc)])
            for j in range(4):
                tt = ti * 4 + j
                x = tt % 2
                for half in range(2):
                    b = cnt % 4; cnt += 1
                    for q in range(4):
                        kc = half * 4 + q
                        T.tr(ps(b)[:, q * 128:(q + 1) * 128], yT[:, kc, j * 128:(j + 1) * 128], ident, reads=['yT', 'ident'], writes=[pk(b)])
                    if half == 0:
                        T.op('dve', lambda e: e.tensor_copy(out=ot[x][:, 0:512], in_=ps(b)), reads=[pk(b)], writes=[('ot', x, 0)])
                    else:
                        T.op('act', lambda e: e.activation(out=ot[x][:, 512:1024], in_=ps(b), func=AF.Copy), reads=[pk(b)], writes=[('ot', x, 1)])
                T.dma('sp', out_d[tt * 128:(tt + 1) * 128, :], ot[x], reads=[('ot', x)], writes=[('out', tt)])

    dbgb = stop or {}
    for li in range(NL):
        with_ctx = li < DEPTH - 1
        if 'with_ctx' in dbgb:
            with_ctx = dbgb['with_ctx']
        adaln(li)
        if not dbgb.get('skip_pre'):
            norm_mod(0, range(5)); T.barrier(keep=KEEP)
            ffn(li, ffn1_up, ffn1_down, 2, range(5)); T.barrier(keep=KEEP)
        norm_mod(1, range(5)); T.barrier(keep=KEEP)
        T.dma('sp', h_spill, hT, reads=['hT'], writes=['h_spill']); T.barrier(keep=tuple(k for k in KEEP if k not in ('hT', 'h_spill')))
        if 'branch' in dbgb:
            br = dbgb['branch']
            fn_ = {0: gqa, 1: hyena, 2: na, 3: mla}[br]
            fn_(li, with_ctx, stage=dbgb.get('stage', 9))
            T.barrier(keep=KEEP)
            d = nc.dram_tensor('dbg_o', [128, 4, TT], BF16, kind="ExternalOutput").ap()
            arena_big()
            tmpo = alloc([4, TT], BF16)
            T.dma('sp', tmpo, o_scr[br], reads=['o_scr'], writes=['tmpo'])
            T.dma('sp', d, tmpo, reads=['tmpo'], writes=['dbg_o'])
            T.finish()
            return nc
        sb_ = dbgb.get('skip_br', ())
        if 0 not in sb_:
            gqa(li, with_ctx); T.barrier(keep=KEEP)
        if 1 not in sb_:
            hyena(li, with_ctx); T.barrier(keep=KEEP)
        if 2 not in sb_:
            na(li, with_ctx); T.barrier(keep=KEEP)
        if 3 not in sb_:
            mla(li, with_ctx); T.barrier(keep=KEEP)
        if not dbgb.get('skip_merge'):
            merge(li, with_ctx); T.barrier(keep=KEEP)
        t2 = range(5) if with_ctx else range(4)
        norm_mod(2, t2); T.barrier(keep=KEEP)
        ffn(li, ffn2_up, ffn2_down, 8, t2); T.barrier(keep=KEEP)
        if dbgb.get('dump_h') == li:
            d = nc.dram_tensor('dbg_h', [128, 8, TT], F32, kind="ExternalOutput").ap()
            T.dma('sp', d, hT, reads=['hT'], writes=['dbg_h'])
    if not dbgb.get('skip_final'):
        final()
    T.finish()
    print('n_ins', T.n_ins, dict(T.ccnt))
    return nc


def kernel(**inputs):
    NL = DEPTH
    nc = build(NL)
    in_maps = [_layout_inputs(inputs, b, NL) for b in range(8)]
    res = run_bass_kernel_spmd(nc, in_maps, core_ids=list(range(8)))
    return np.stack([np.asarray(r['out'], dtype=np.float32) for r in res.results], 0)
```
